# Optimizing a Trainium2 kernel written in Bass

```python
import jax, jax.numpy as jnp
from jax import lax
import numpy as np

D_MODEL = 2048
BATCH = 16
SEQ = 256
DEPTH = 4
DEC_BATCH = 4
DEC_SEQ = 2048
PAST_LEN = 256

GRID_W = 64
D_MIX = D_MODEL
D_FOURIER = D_MIX // 4
N_FOURIER_GROUPS = 4
FOURIER_GROUP = D_FOURIER // N_FOURIER_GROUPS
D_DELTA = D_MIX - D_FOURIER
HEAD_DIM = 128
N_HEADS = D_DELTA // HEAD_DIM
N_DIR = 2
CONV_K = 3
CHUNK = 64
D_IN = 2 * D_FOURIER + 4 * D_DELTA + 2 * N_DIR * N_HEADS
EPS = 1e-6

kernel_name = "hybrid_fourier_gdn_diffusion_step"


def _rms_norm(x, w):
    xf = x.astype(jnp.float32)
    y = xf * lax.rsqrt(jnp.mean(xf * xf, axis=-1, keepdims=True) + EPS)
    return (y * w.astype(jnp.float32)).astype(x.dtype)


def _l2norm(x):
    return x * lax.rsqrt(jnp.sum(x * x, axis=-1, keepdims=True) + EPS)


def _short_conv(x, w):
    pad = CONV_K // 2
    l = x.shape[1]
    xp = jnp.pad(x, ((0, 0), (pad, pad), (0, 0)))
    y = xp[:, 0:l] * w[0]
    for j in range(1, CONV_K):
        y = y + xp[:, j:j + l] * w[j]
    return jax.nn.silu(y)


def _fourier_mix(u, grid):
    b, l, _ = u.shape
    uf = u.astype(jnp.float32)
    if grid:
        rows = l // GRID_W
        uf = uf.reshape(b, rows, GRID_W, N_FOURIER_GROUPS, FOURIER_GROUP)
        y = jnp.fft.fftn(uf, axes=(1, 2, 4), norm="ortho").real
    else:
        uf = uf.reshape(b, l, N_FOURIER_GROUPS, FOURIER_GROUP)
        y = jnp.fft.fftn(uf, axes=(1, 3), norm="ortho").real
    return y.reshape(b, l, D_FOURIER).astype(u.dtype)


def _chunk_gated_delta(q, k, v, log_g, beta, s0):
    b, l, h, _ = q.shape
    dv = v.shape[-1]
    n = l // CHUNK

    def heads_first(t):
        t = t.reshape((b, n, CHUNK, h) + t.shape[3:])
        return jnp.moveaxis(t, 3, 1)

    q, k, v, log_g, beta = (heads_first(t) for t in (q, k, v, log_g, beta))
    g = jnp.cumsum(log_g, axis=-1)
    idx = jnp.arange(CHUNK)
    incl = idx[:, None] >= idx[None, :]
    strict = idx[:, None] > idx[None, :]
    diff = g[..., :, None] - g[..., None, :]
    decay = jnp.where(incl, jnp.exp(jnp.where(incl, diff, 0.0)), 0.0)
    kb = k * beta[..., None]
    vb = v * beta[..., None]
    lmat = jnp.where(strict, jnp.einsum("bhncd,bhnsd->bhncs", kb, k) * decay, 0.0)
    eye = jnp.eye(CHUNK, dtype=q.dtype)
    a = lmat + eye
    t_inv = lax.linalg.triangular_solve(a, jnp.broadcast_to(eye, a.shape),
                                        left_side=True, lower=True)
    u = jnp.einsum("bhncs,bhnsd->bhncd", t_inv, vb)
    w = jnp.einsum("bhncs,bhnsd->bhncd", t_inv, kb * jnp.exp(g)[..., None])
    attn = jnp.where(incl, jnp.einsum("bhncd,bhnsd->bhncs", q, k) * decay, 0.0)
    qg = q * jnp.exp(g)[..., None]
    g_last = g[..., -1]
    kd = k * jnp.exp(g_last[..., None] - g)[..., None]
    xs = tuple(jnp.moveaxis(t, 2, 0) for t in (u, w, attn, qg, kd, g_last))

    def step(s, inp):
        u_i, w_i, a_i, qg_i, kd_i, gl_i = inp
        v_new = u_i - jnp.einsum("bhcd,bhde->bhce", w_i, s)
        o_i = (jnp.einsum("bhcd,bhde->bhce", qg_i, s)
               + jnp.einsum("bhcs,bhse->bhce", a_i, v_new))
        s = s * jnp.exp(gl_i)[..., None, None] + jnp.einsum("bhcd,bhce->bhde", kd_i, v_new)
        return s, o_i

    s_fin, o = lax.scan(step, s0, xs)
    o = jnp.transpose(o, (1, 0, 3, 2, 4)).reshape(b, l, h, dv)
    return o, s_fin


def _layer(x, mod, norm_w, w_in, conv_w, a_log, dt_bias, gnorm_w, w_out, s0, grid):
    bsz, l, _ = x.shape
    shift, scale, gate = jnp.split(mod, 3, axis=-1)
    h = _rms_norm(x, norm_w) * (1.0 + scale[:, None]) + shift[:, None]
    proj = h @ w_in
    u_f, z_f, qkv, z_d, ab = jnp.split(
        proj, [D_FOURIER, 2 * D_FOURIER, 2 * D_FOURIER + 3 * D_DELTA,
               2 * D_FOURIER + 4 * D_DELTA], axis=-1)
    y_f = _fourier_mix(u_f, grid) * jax.nn.silu(z_f)
    qkv = _short_conv(qkv, conv_w).astype(jnp.float32)
    q, k, v = jnp.split(qkv, 3, axis=-1)
    q = _l2norm(q.reshape(bsz, l, N_HEADS, HEAD_DIM)) * (HEAD_DIM ** -0.5)
    k = _l2norm(k.reshape(bsz, l, N_HEADS, HEAD_DIM))
    v = v.reshape(bsz, l, N_HEADS, HEAD_DIM)
    ab = ab.astype(jnp.float32).reshape(bsz, l, 2, N_DIR, N_HEADS)
    log_g = -jnp.exp(a_log.astype(jnp.float32)) * jax.nn.softplus(
        ab[:, :, 0] + dt_bias.astype(jnp.float32))
    beta = jax.nn.sigmoid(ab[:, :, 1])
    s0 = s0.astype(jnp.float32)
    o_fw, s_fw = _chunk_gated_delta(q, k, v, log_g[:, :, 0], beta[:, :, 0], s0[:, 0])
    flip = lambda t: jnp.flip(t, axis=1)
    o_bw, s_bw = _chunk_gated_delta(flip(q), flip(k), flip(v), flip(log_g[:, :, 1]),
                                    flip(beta[:, :, 1]), s0[:, 1])
    o = (o_fw + flip(o_bw)).astype(x.dtype)
    y_d = _rms_norm(o, gnorm_w).reshape(bsz, l, D_DELTA) * jax.nn.silu(z_d)
    y = jnp.concatenate([y_f, y_d], axis=-1) @ w_out
    return x + gate[:, None] * y, jnp.stack([s_fw, s_bw], axis=1)


def setup_inputs(seed: int = 0) -> dict:
    key = jax.random.key(seed)
    ks = jax.random.split(key, 16)
    f32 = jnp.float32
    x_prompt = jax.random.normal(ks[0], (BATCH, SEQ, D_MODEL), f32)
    x_sample = jax.random.normal(ks[1], (DEC_BATCH, DEC_SEQ, D_MODEL), f32)
    state_ctx = 0.1 * jax.random.normal(
        ks[2], (DEC_BATCH, DEPTH, N_DIR, N_HEADS, HEAD_DIM, HEAD_DIM), f32)
    c = jax.random.normal(ks[3], (DEC_BATCH, D_MODEL), f32)
    c_ctx = jax.random.normal(ks[4], (D_MODEL,), f32)
    norm_w = 1.0 + 0.02 * jax.random.normal(ks[5], (DEPTH, D_MODEL), f32)
    w_mod = 0.5 * D_MODEL ** -0.5 * jax.random.normal(ks[6], (DEPTH, D_MODEL, 3 * D_MODEL), f32)
    b_mod = 0.02 * jax.random.normal(ks[7], (DEPTH, 3 * D_MODEL), f32)
    w_in = D_MODEL ** -0.5 * jax.random.normal(ks[8], (DEPTH, D_MODEL, D_IN), f32)
    conv_w = CONV_K ** -0.5 * jax.random.normal(ks[9], (DEPTH, CONV_K, 3 * D_DELTA), f32)
    a_log = jnp.log(jax.random.uniform(ks[10], (DEPTH, N_DIR, N_HEADS), f32, 1.0, 16.0))
    dt_bias = 0.5 * jax.random.normal(ks[11], (DEPTH, N_DIR, N_HEADS), f32)
    gnorm_w = 1.0 + 0.02 * jax.random.normal(ks[12], (DEPTH, HEAD_DIM), f32)
    w_out = D_MIX ** -0.5 * jax.random.normal(ks[13], (DEPTH, D_MIX, D_MODEL), f32)
    final_norm_w = 1.0 + 0.02 * jax.random.normal(ks[14], (D_MODEL,), f32)
    return {"x_prompt": x_prompt, "x_sample": x_sample, "state_ctx": state_ctx,
            "c": c, "c_ctx": c_ctx, "norm_w": norm_w, "w_mod": w_mod, "b_mod": b_mod,
            "w_in": w_in, "conv_w": conv_w, "a_log": a_log, "dt_bias": dt_bias,
            "gnorm_w": gnorm_w, "w_out": w_out, "final_norm_w": final_norm_w}


def reference(x_prompt, x_sample, state_ctx, c, c_ctx, norm_w, w_mod, b_mod, w_in,
              conv_w, a_log, dt_bias, gnorm_w, w_out, final_norm_w):
    xp = x_prompt
    xs = x_sample
    s_zero = jnp.zeros((x_prompt.shape[0], N_DIR, N_HEADS, HEAD_DIM, HEAD_DIM), jnp.float32)
    silu_ctx = jax.nn.silu(c_ctx)[None]
    silu_c = jax.nn.silu(c)
    states = []
    for i in range(DEPTH):
        mod_ctx = silu_ctx @ w_mod[i] + b_mod[i]
        mod_lat = silu_c @ w_mod[i] + b_mod[i]
        xp, s_new = _layer(xp, mod_ctx, norm_w[i], w_in[i], conv_w[i], a_log[i],
                           dt_bias[i], gnorm_w[i], w_out[i], s_zero, False)
        states.append(s_new)
        xs, _ = _layer(xs, mod_lat, norm_w[i], w_in[i], conv_w[i], a_log[i],
                       dt_bias[i], gnorm_w[i], w_out[i], state_ctx[:, i], True)
    y_prompt = _rms_norm(xp, final_norm_w)
    y_sample = _rms_norm(xs, final_norm_w)
    state_new = jnp.stack(states, axis=1).astype(x_prompt.dtype)
    return (y_prompt, y_sample, state_new)
```

```python
import numpy as np
import ml_dtypes
from contextlib import ExitStack
import concourse.bass as bass
import concourse.mybir as mybir
from concourse.bass_utils import run_bass_kernel_spmd

F32 = mybir.dt.float32
BF16 = mybir.dt.bfloat16
AF = mybir.ActivationFunctionType
ALU = mybir.AluOpType
AX = mybir.AxisListType

SAME_ENGINE_WAITS = True
N_DMA_SEMS = 40
N_SW_SEMS = 8


class Trk:
    __slots__ = ("w", "r", "name", "excl")

    def __init__(self, name="", excl=False):
        self.w = {}
        self.r = {}
        self.name = name
        self.excl = excl


class Prog:
    ENGS = ("pe", "act", "dve", "pool", "sp")

    def __init__(self, nc):
        self.nc = nc
        self.ops = {e: [] for e in self.ENGS}
        self.sem = {e: nc.alloc_semaphore("sem_" + e) for e in self.ENGS if e != "sp"}
        self.cnt = {e: 0 for e in self.ENGS}
        self.known = {e: {} for e in self.ENGS}
        self.dsem = [nc.alloc_semaphore("dsem%d" % i) for i in range(N_DMA_SEMS)]
        self.dcnt = [0] * N_DMA_SEMS
        self.drr = 0
        self.drr_sw = 0
        self.stack = ExitStack()
        self.nalloc = 0

    def sb(self, shape, dtype, name=None):
        self.nalloc += 1
        name = name or ("sb%d" % self.nalloc)
        t = self.stack.enter_context(self.nc.sbuf_tensor(name, list(shape), dtype))
        return t

    def ps(self, shape, dtype, name=None):
        self.nalloc += 1
        name = name or ("ps%d" % self.nalloc)
        t = self.stack.enter_context(self.nc.psum_tensor(name, list(shape), dtype))
        return t

    def _need(self, e, reads, writes, joins=()):
        need = {}
        for t in reads:
            for k, v in t.w.items():
                if need.get(k, (None, 0))[1] < v[1]:
                    need[k] = v
            if t.excl:
                for k, v in t.r.items():
                    if k != e and need.get(k, (None, 0))[1] < v[1]:
                        need[k] = v
        for t in writes:
            for d in (t.w, t.r):
                for k, v in d.items():
                    if need.get(k, (None, 0))[1] < v[1]:
                        need[k] = v
        for t in joins:
            for k, v in t.r.items():
                if need.get(k, (None, 0))[1] < v[1]:
                    need[k] = v
        kn = self.known[e]
        waits = []
        for k, (sem, val) in need.items():
            if kn.get(k, 0) >= val:
                continue
            if k == e and (not SAME_ENGINE_WAITS or e == "pe" or val > self.cnt[e]):
                continue
            kn[k] = val
            waits.append((sem, val))
        return waits

    def emit(self, e, fn, reads=(), writes=(), inc=True):
        waits = self._need(e, reads, writes)
        val = self.cnt[e] + 1
        if inc:
            self.cnt[e] = val
        ev = (self.sem[e], val)
        self.ops[e].append((fn, waits, inc))
        for t in reads:
            t.r[e] = ev
        for t in writes:
            t.w = {e: ev}
            t.r = {}

    def dma(self, q, out_ap, in_ap, reads=(), writes=(), joins=()):
        if q == "pool":
            i = N_DMA_SEMS - N_SW_SEMS + self.drr_sw
            self.drr_sw = (self.drr_sw + 1) % N_SW_SEMS
        else:
            i = self.drr
            self.drr = (i + 1) % (N_DMA_SEMS - N_SW_SEMS)
        sem = self.dsem[i]
        key = ("d", i)
        waits = self._need(q, reads, writes, joins)
        prev = self.dcnt[i]
        if prev > 0 and self.known[q].get(key, 0) < prev:
            self.known[q][key] = prev
            waits.append((sem, prev))
        self.dcnt[i] = prev + 16
        ev = (sem, prev + 16)

        def fn(eng, out_ap=out_ap, in_ap=in_ap, sem=sem):
            return eng.dma_start(out=out_ap, in_=in_ap).then_inc(sem, 16)

        self.ops[q].append((fn, waits, False))
        for t in reads:
            t.r[key] = ev
        for t in writes:
            t.w = {key: ev}
            t.r = {}
        for t in joins:
            t.w[key] = ev
            t.r = {}

    def mm(self, out, lhsT, rhs, start=True, stop=True, reads=(), writes=(), inc=None):
        if inc is None:
            inc = stop
        self.emit("pe", lambda eng: eng.matmul(out, lhsT, rhs, start=start, stop=stop),
                  reads, writes, inc)

    def tr(self, out, in_, ident, reads=(), writes=(), inc=True):
        self.emit("pe", lambda eng: eng.transpose(out, in_, ident), reads, writes, inc)

    def act(self, out, in_, func, bias=0.0, scale=1.0, reads=(), writes=()):
        self.emit("act", lambda eng: eng.activation(out, in_, func, bias=bias, scale=scale),
                  reads, writes)

    def tt(self, e, out, in0, in1, op, reads=(), writes=()):
        self.emit(e, lambda eng: eng.tensor_tensor(out, in0, in1, op), reads, writes)

    def ts(self, e, out, in0, s1, s2, op0, op1=None, reads=(), writes=()):
        if op1 is None:
            self.emit(e, lambda eng: eng.tensor_scalar(out, in0, s1, None, op0), reads, writes)
        else:
            self.emit(e, lambda eng: eng.tensor_scalar(out, in0, s1, s2, op0, op1), reads, writes)

    def stt(self, out, in0, scalar, in1, op0, op1, reads=(), writes=()):
        self.emit("dve", lambda eng: eng.scalar_tensor_tensor(out, in0, scalar, in1, op0, op1),
                  reads, writes)

    def copy(self, e, out, in_, reads=(), writes=()):
        import os
        if e == "act":
            if os.environ.get("ALT_ACT"):
                self.emit("act", lambda eng: eng.activation(out, in_, AF.Identity), reads, writes)
            else:
                self.emit("act", lambda eng: eng.copy(out, in_), reads, writes)
        else:
            if os.environ.get("ALT_DVE") and e == "dve":
                self.emit(e, lambda eng: eng.tensor_scalar(out, in_, 1.0, None, ALU.mult), reads, writes)
            else:
                self.emit(e, lambda eng: eng.tensor_copy(out, in_), reads, writes)

    def memset(self, e, ap, val, writes=()):
        self.emit(e, lambda eng: eng.memset(ap, val), (), writes)

    def recip(self, out, in_, reads=(), writes=()):
        self.emit("dve", lambda eng: eng.reciprocal(out, in_), reads, writes)

    def reduce_sum(self, out, in_, reads=(), writes=()):
        self.emit("dve", lambda eng: eng.reduce_sum(out, in_, AX.X), reads, writes)

    def finish(self):
        fw = []
        for i in range(N_DMA_SEMS):
            if self.dcnt[i] > 0:
                fw.append((self.dsem[i], self.dcnt[i]))
        for e in ("pe", "act", "dve", "pool"):
            if self.cnt[e] > 0:
                fw.append((self.sem[e], self.cnt[e]))
        self.final_waits = fw

    def build(self):
        nc = self.nc
        self.finish()
        prog = self

        def replay(e, eng):
            for fn, waits, inc in prog.ops[e]:
                for sem, val in waits:
                    eng.wait_ge(sem, val)
                ins = fn(eng)
                if inc:
                    ins.then_inc(prog.sem[e], 1)

        with nc.Block() as block:
            @block.tensor
            def _(eng):
                replay("pe", eng)

            @block.scalar
            def _(eng):
                replay("act", eng)

            @block.vector
            def _(eng):
                replay("dve", eng)

            @block.gpsimd
            def _(eng):
                replay("pool", eng)

            @block.sync
            def _(eng):
                replay("sp", eng)
                for sem, val in prog.final_waits:
                    eng.wait_ge(sem, val)
        self.stack.close()


D = 2048
KC = 16
NH = 12
DIN = 7216
EPS = 1e-6
GROUPS = {
    "p": dict(T=512, nseq=2, L=256, grid=False),
    "s": dict(T=2048, nseq=1, L=2048, grid=True),
}
TMAX = 2048
NCMAX = 16
R_NORM, R_BMOD, R_CONV, R_GN, R_FN, R_CCTX, R_C = 0, 64, 256, 688, 692, 708, 724


class Rot:
    def __init__(self, items):
        self.items = items
        self.i = 0

    def next(self):
        it = self.items[self.i]
        self.i = (self.i + 1) % len(self.items)
        return it


class _Stop(Exception):
    pass


def build_program(depth=4, groups=("p", "s")):
    import os
    nc = bass.Bass("TRN2", target_bir_lowering=False)
    P = Prog(nc)
    KSTOP = int(os.environ.get("KSTOP", "0"))

    def chk(n):
        if KSTOP == n:
            raise _Stop()

    def din(name, shape, dt=F32):
        return nc.dram_tensor(name, list(shape), dt, kind="ExternalInput").ap()

    def dout(name, shape, dt=F32):
        return nc.dram_tensor(name, list(shape), dt, kind="ExternalOutput").ap()

    def dscr(name, shape, dt=F32):
        return nc.dram_tensor(name, list(shape), dt, kind="Internal").ap()

    xin = {g: din("xT_" + g, [KC, 128, GROUPS[g]["T"]]) for g in groups}
    yout = {g: dout("yT_" + g, [KC, 128, GROUPS[g]["T"]]) for g in groups}
    xscr = {g: dscr("xscr_" + g, [KC, 128, GROUPS[g]["T"]]) for g in groups}
    yscr = dscr("yscr", [KC, 128, TMAX], BF16)
    w_in = din("w_in", [depth, D, DIN])
    w_out = din("w_out", [depth, D, D])
    w_mod = din("w_mod", [depth, D, 3 * D])
    small = din("small", [768, 128])
    albt = din("albt", [128, 192])
    cf32 = din("cf32", [128, 4, 128])
    masks = din("masks", [128, 4, 128])
    cbf = din("cbf", [128, 2, 128], BF16)
    dftc = din("dftc", [128, 2, 128], BF16)
    dftp = din("dftp", [128, 2, 512], BF16)
    dfts = din("dfts", [128, 3, 256], BF16)
    if "s" in groups:
        s0_in = din("s0", [depth, 2, NH, 128, 128])
    if "p" in groups:
        st_out = dout("st", [2, depth, 2, NH, 128, 128])

    hT = P.sb([128, KC, TMAX], BF16, "hT")
    t_hT = [Trk() for _ in range(4)]
    NW = 4
    wring = [P.sb([128, KC, 128], BF16, "wr%d" % i) for i in range(NW)]
    t_wring = [Trk() for _ in range(NW)]
    wrot = Rot(list(range(NW)))
    raw = P.sb([128, TMAX], F32, "raw"); t_raw = Trk()
    raw_main = raw
    cv = P.sb([128, TMAX], F32, "cv"); t_cv = Trk()
    abt = raw[:, 0:NCMAX * 48].rearrange("p (a c) -> p a c", c=48); t_abt = t_raw
    kqf = P.sb([128, 2 * TMAX], BF16, "kqf"); t_kq = Trk()
    t_qT = t_kq; t_kT = t_kq
    qT = kqf[:, 0:TMAX]
    kT = kqf[:, TMAX:2 * TMAX]

    def kch(c):
        return kqf[:, c * 256:c * 256 + 128]

    def qch(c):
        return kqf[:, c * 256 + 128:(c + 1) * 256]
    vT = P.sb([128, TMAX], BF16, "vT"); t_vT = Trk()
    zs = P.sb([128, TMAX], BF16, "zs"); t_zs = Trk()
    kvar = [P.sb([128, NCMAX, 128], BF16, "kvar%d" % i) for i in range(4)]
    t_kvar = [Trk() for _ in range(4)]
    vtok = P.sb([128, NCMAX, 128], BF16, "vtok"); t_vtok = Trk()
    obuf = P.sb([128, NCMAX, 128], F32, "obuf"); t_obuf = [Trk() for _ in range(NCMAX)]
    scrB = P.sb([128, TMAX], BF16, "scrB"); t_scrB = Trk()
    rstd = scrB[:, 0:1024].bitcast(F32); t_rstd = t_scrB
    S = P.sb([128, 24, 128], F32, "S"); Sb = P.sb([128, 24, 128], BF16, "Sb")
    t_S = [Trk() for _ in range(24)]; t_Sb = [Trk() for _ in range(24)]
    lg = P.sb([128, NCMAX, 24], F32, "lg"); t_lg = Trk()
    nbeta = P.sb([128, NCMAX, 24], F32, "nbeta"); t_nbeta = Trk()
    gcs = P.sb([128, NCMAX, 24], F32, "gcs"); t_gcs = Trk()
    eg = P.sb([128, NCMAX, 24], F32, "eg"); t_eg = Trk()
    edl = P.sb([128, NCMAX, 24], F32, "edl"); t_edl = Trk()
    egl = P.sb([128, NCMAX, 24], F32, "egl"); t_egl = Trk()
    sqs = [P.sb([128, 512], BF16, "sq%d" % i) for i in range(2)]; t_sqs = [Trk(), Trk()]
    sqrot = Rot([0, 1])
    smT = P.sb([128, 768], F32, "smT"); t_smT = Trk()
    albc = P.sb([128, 192], F32, "albc"); t_albc = Trk()
    nea = P.sb([128, 96], F32, "nea"); t_nea = Trk()
    scT = P.sb([128, KC, 2], BF16, "scT"); t_scT = Trk()
    modT = P.sb([128, depth, 48, 2], F32, "modT"); t_modT = Trk()
    Amod = P.sb([128, depth, 2, KC], F32, "Amod"); t_Amod = Trk()
    c32 = P.sb([128, 4, 128], F32, "c32"); t_c32 = Trk()
    msk = P.sb([128, 4, 128], F32, "msk"); t_msk = Trk()
    cb = P.sb([128, 2, 128], BF16, "cb"); t_cb = Trk()
    dc = P.sb([128, 2, 128], BF16, "dc"); t_dc = Trk()
    dpos = P.sb([128, 1024], BF16, "dpos"); t_dp = Trk(); t_dsm = t_dp
    dp = dpos[:, 0:1024].rearrange("p (a c) -> p a c", c=512)
    dsm = dpos[:, 0:768].rearrange("p (a c) -> p a c", c=256)
    def mk(n, dt, name, k=2):
        return Rot([(P.sb([128, n], dt, "%s%d" % (name, i)), Trk()) for i in range(k)])
    def mkU(n, dt, name, k=4):
        return [(P.sb([128, n], dt, "%s%d" % (name, i)), Trk()) for i in range(k)]
    U_Dm = mkU(128, F32, "uDm"); U_Es = mkU(128, F32, "uEs"); U_Ei = mkU(128, F32, "uEi")
    U_PRt = [P.sb([128, 256], BF16, "uPR%d" % i) for i in range(4)]
    U_P = [(U_PRt[i][:, 0:128], Trk()) for i in range(4)]
    U_R = [(U_PRt[i][:, 128:256], Trk()) for i in range(4)]
    U_PT = mkU(128, BF16, "uPT")
    U_At = [mkU(128, BF16, "uAt%d" % p) for p in range(2)]
    U_upb = [mkU(128, BF16, "uupb%d" % p) for p in range(2)]
    U_wpT = [mkU(128, BF16, "uwpT%d" % p) for p in range(2)]
    r_P = Rot([(U_PRt[i], U_P[i][1]) for i in range(4)]); r_PT = Rot(U_PT)
    r_vn = mk(128, BF16, "vn"); r_o1 = mk(128, F32, "o1")
    r_st = mk(512, F32, "stg", 2)
    ssq = P.sb([128, NCMAX], F32, "ssq"); t_ssq = Trk()

    banks = [P.ps([128, 512], F32, "bank%d" % i) for i in range(8)]
    t_bank = [Trk(excl=True) for _ in range(8)]
    big = Rot([0, 1, 2, 3])
    sml = Rot([4, 5, 6, 7, 0, 1, 2, 3])

    ident_f = c32[:, 0, :]
    U_f = c32[:, 1, :]
    L_f = c32[:, 2, :]
    ones_f = c32[:, 3, :]
    ident_b = cb[:, 0, :]
    ones_b = cb[:, 1, :]

    P.dma("sp", c32[:], cf32, writes=[t_c32])
    P.dma("sp", msk[:], masks, writes=[t_msk])
    P.dma("sp", cb[:], cbf, writes=[t_cb])
    P.dma("sp", dc[:], dftc, writes=[t_dc])
    P.dma("sp", albc[:], albt, writes=[t_albc])
    stg6 = cv[:, 0:768].rearrange("p (a c) -> p a c", a=6)
    P.dma("sp", stg6, small.rearrange("(a p) c -> p a c", p=128), writes=[t_cv])
    for a in range(6):
        b = sml.next()
        P.tr(banks[b][:, 0:128], stg6[:, a, :], ident_f, reads=[t_cv, t_c32], writes=[t_bank[b]])
        P.copy("dve", smT[:, a * 128:(a + 1) * 128], banks[b][:, 0:128], reads=[t_bank[b]], writes=[t_smT])
    P.act(nea[:], albc[:, 0:96], AF.Exp, reads=[t_albc], writes=[t_nea])
    P.ts("dve", nea[:], nea[:], -1.0, None, ALU.mult, reads=[t_nea], writes=[t_nea])
    P.act(scT[:, :, 0], smT[:, R_CCTX:R_CCTX + 16], AF.Silu, reads=[t_smT], writes=[t_scT])
    P.act(scT[:, :, 1], smT[:, R_C:R_C + 16], AF.Silu, reads=[t_smT, t_scT], writes=[t_scT])

    try:
        chk(1)
    except _Stop:
        P.build(); return nc

    def load_w(src2d, ncols=128):
        i = wrot.next()
        P.dma("pool", wring[i][:, :, 0:ncols], src2d.rearrange("(kc p) n -> p kc n", p=128),
              writes=[t_wring[i]])
        return i

    for l in range(depth):
        b = sml.next()
        for blk in range(48):
            wi = load_w(w_mod[l, :, blk * 128:(blk + 1) * 128])
            for kc in range(KC):
                P.mm(banks[b][:, blk * 2:blk * 2 + 2], wring[wi][:, kc, :], scT[:, kc, :],
                     start=(kc == 0), stop=(kc == KC - 1),
                     reads=[t_wring[wi], t_scT], writes=[t_bank[b]])
        P.tt("dve", modT[:, l, :, :], banks[b][:, 0:96].rearrange("p (a c) -> p a c", c=2),
             smT[:, R_BMOD + l * 48:R_BMOD + (l + 1) * 48].unsqueeze(2).to_broadcast([128, 48, 2]),
             ALU.add, reads=[t_bank[b], t_smT], writes=[t_modT])
        for gi in range(2):
            P.ts("dve", Amod[:, l, gi, :], modT[:, l, 16:32, gi], 1.0, None, ALU.add,
                 reads=[t_modT], writes=[t_Amod])
            P.tt("dve", Amod[:, l, gi, :], Amod[:, l, gi, :], smT[:, R_NORM + l * 16:R_NORM + (l + 1) * 16],
                 ALU.mult, reads=[t_Amod, t_smT], writes=[t_Amod])

    try:
        chk(2)
    except _Stop:
        P.build(); return nc

    def rsqrt_from_bank(out_ap, t_out, b, ncols, scale, reads_extra=()):
        P.act(out_ap, banks[b][:, 0:ncols], AF.Sqrt, bias=EPS, scale=scale,
              reads=[t_bank[b]], writes=[t_out])
        P.recip(out_ap, out_ap, reads=[t_out], writes=[t_out])

    def norm_pass(g, src, t_src, l, final):
        G = GROUPS[g]; T = G["T"]; NT = T // 512
        gi = 0 if g == "p" else 1
        for nt in range(NT):
            tsl = slice(nt * 512, (nt + 1) * 512)
            b = big.next()
            for kc in range(KC):
                st, t_st = r_st.next()
                P.dma("sp", st[:], src[kc, :, tsl], reads=[t_src[kc][nt]], writes=[t_st])
                si = sqrot.next()
                P.act(sqs[si][:], st[:], AF.Square, reads=[t_st], writes=[t_sqs[si]])
                P.mm(banks[b][:], ones_b, sqs[si][:], start=(kc == 0), stop=(kc == KC - 1),
                     reads=[t_sqs[si], t_cb], writes=[t_bank[b]], inc=True)
            rsqrt_from_bank(rstd, t_rstd, b, 512, 1.0 / D)
            for kc in range(KC):
                st, t_st = r_st.next()
                P.dma("sp", st[:], src[kc, :, tsl], reads=[t_src[kc][nt]], writes=[t_st])
                P.tt("dve", st[:], st[:], rstd, ALU.mult, reads=[t_st, t_rstd], writes=[t_st])
                if not final:
                    P.act(hT[:, kc, tsl], st[:], AF.Identity,
                          bias=modT[:, l, kc, gi:gi + 1], scale=Amod[:, l, gi, kc:kc + 1],
                          reads=[t_st, t_modT, t_Amod], writes=[t_hT[nt]])
                else:
                    P.act(st[:], st[:], AF.Identity, scale=smT[:, R_FN + kc:R_FN + kc + 1],
                          reads=[t_st, t_smT], writes=[t_st])
                    P.dma("sp", yout[g][kc, :, tsl], st[:], reads=[t_st])

    def proj(T, wi, evac):
        NT = T // 512
        for nt in range(NT):
            b = big.next()
            for kc in range(KC):
                P.mm(banks[b][:], wring[wi][:, kc, :], hT[:, kc, nt * 512:(nt + 1) * 512],
                     start=(kc == 0), stop=(kc == KC - 1),
                     reads=[t_wring[wi], t_hT[nt]], writes=[t_bank[b]])
            evac(nt, b)

    def conv_silu(G, l, blk, out_ap, t_out, out_is_cv, raw=None, t_raws=None):
        T, L, nseq = G["T"], G["L"], G["nseq"]
        if raw is None:
            raw = raw_main; t_raws = [t_raw]
        base = R_CONV + l * 108
        w0 = smT[:, base + blk:base + blk + 1]
        w1 = smT[:, base + 36 + blk:base + 36 + blk + 1]
        w2 = smT[:, base + 72 + blk:base + 72 + blk + 1]
        P.ts("dve", cv[:, 0:T], raw[:, 0:T], w1, None, ALU.mult, reads=t_raws + [t_smT], writes=[t_cv])
        for s in range(nseq):
            a, e = s * L, (s + 1) * L
            P.stt(cv[:, a + 1:e], raw[:, a:e - 1], w0, cv[:, a + 1:e], ALU.mult, ALU.add,
                  reads=t_raws + [t_smT, t_cv], writes=[t_cv])
            P.stt(cv[:, a:e - 1], raw[:, a + 1:e], w2, cv[:, a:e - 1], ALU.mult, ALU.add,
                  reads=t_raws + [t_smT, t_cv], writes=[t_cv])
        if out_is_cv:
            P.act(cv[:, 0:T], cv[:, 0:T], AF.Silu, reads=[t_cv], writes=[t_cv])
        else:
            P.act(out_ap, cv[:, 0:T], AF.Silu, reads=[t_cv], writes=[t_out])

    def l2norm(T, out_ap, t_out, scale):
        NT = T // 512
        P.act(scrB[:, 0:T], cv[:, 0:T], AF.Square, reads=[t_cv], writes=[t_scrB])
        for nt in range(NT):
            tsl = slice(nt * 512, (nt + 1) * 512)
            b = big.next()
            P.mm(banks[b][:], ones_b, scrB[:, tsl], reads=[t_scrB, t_cb], writes=[t_bank[b]])
            st, t_st = r_st.next()
            P.act(st[:], banks[b][:], AF.Sqrt, bias=EPS, scale=1.0, reads=[t_bank[b]], writes=[t_st])
            P.recip(st[:], st[:], reads=[t_st], writes=[t_st])
            o3 = kqf[:, nt * 1024:(nt + 1) * 1024].rearrange("p (a c) -> p a c", c=256)
            o3 = o3[:, :, 128:256] if out_ap == "q" else o3[:, :, 0:128]
            P.stt(o3, cv[:, tsl].rearrange("p (a c) -> p a c", c=128), scale,
                  st[:].rearrange("p (a c) -> p a c", c=128), ALU.mult, ALU.mult,
                  reads=[t_cv, t_st], writes=[t_out])

    def layer(g, l, src, t_src, dst, t_dst):
        G = GROUPS[g]; T, L, nseq, grid = G["T"], G["L"], G["nseq"], G["grid"]
        NT = T // 512; NC = T // 128; NCs = L // 128
        gi = 0 if g == "p" else 1
        W = w_in[l]
        norm_pass(g, src, t_src, l, False)
        chk(3)
        wi = load_w(W[:, 7168:7216], 48)
        for c0 in range(0, NC, 8):
            b = sml.next()
            for c in range(c0, min(NC, c0 + 8)):
                for kc in range(KC):
                    P.mm(banks[b][:, (c - c0) * 48:(c - c0 + 1) * 48], hT[:, kc, c * 128:(c + 1) * 128],
                         wring[wi][:, kc, 0:48], start=(kc == 0), stop=(kc == KC - 1),
                         reads=[t_wring[wi], t_hT[c // 4]], writes=[t_bank[b]])
            n = min(NC, c0 + 8) - c0
            P.copy("dve", abt[:, c0:c0 + n, :], banks[b][:, 0:n * 48].rearrange("p (a c) -> p a c", c=48),
                   reads=[t_bank[b]], writes=[t_abt])
        dtb = albc[:, 96 + l * 24:96 + (l + 1) * 24].unsqueeze(1).to_broadcast([128, NC, 24])
        neab = nea[:, l * 24:(l + 1) * 24].unsqueeze(1).to_broadcast([128, NC, 24])
        P.tt("dve", lg[:, 0:NC, :], abt[:, 0:NC, 0:24], dtb, ALU.add, reads=[t_abt, t_albc], writes=[t_lg])
        P.act(lg[:, 0:NC, :], lg[:, 0:NC, :], AF.Exp, reads=[t_lg], writes=[t_lg])
        P.act(lg[:, 0:NC, :], lg[:, 0:NC, :], AF.Ln, bias=1.0, reads=[t_lg], writes=[t_lg])
        P.tt("dve", lg[:, 0:NC, :], lg[:, 0:NC, :], neab, ALU.mult, reads=[t_lg, t_nea], writes=[t_lg])
        P.act(nbeta[:, 0:NC, :], abt[:, 0:NC, 24:48], AF.Exp, scale=-1.0, reads=[t_abt], writes=[t_nbeta])
        P.ts("dve", nbeta[:, 0:NC, :], nbeta[:, 0:NC, :], 1.0, None, ALU.add, reads=[t_nbeta], writes=[t_nbeta])
        P.recip(nbeta[:, 0:NC, :], nbeta[:, 0:NC, :], reads=[t_nbeta], writes=[t_nbeta])
        P.ts("dve", nbeta[:, 0:NC, :], nbeta[:, 0:NC, :], -1.0, None, ALU.mult, reads=[t_nbeta], writes=[t_nbeta])
        for c0 in range(0, NC, 16):
            b = sml.next(); b2 = sml.next()
            for c in range(c0, min(NC, c0 + 16)):
                o = (c - c0) * 24
                P.mm(banks[b][:, o:o + 12], U_f, lg[:, c, 0:12], reads=[t_c32, t_lg], writes=[t_bank[b]])
                P.mm(banks[b][:, o + 12:o + 24], L_f, lg[:, c, 12:24], reads=[t_c32, t_lg], writes=[t_bank[b]])
                P.mm(banks[b2][:, o:o + 24], ones_f, lg[:, c, :], reads=[t_c32, t_lg], writes=[t_bank[b2]])
            n = min(NC, c0 + 16) - c0
            P.copy("dve", gcs[:, c0:c0 + n, :], banks[b][:, 0:n * 24].rearrange("p (a c) -> p a c", c=24),
                   reads=[t_bank[b]], writes=[t_gcs])
            P.tt("dve", edl[:, c0:c0 + n, :], banks[b2][:, 0:n * 24].rearrange("p (a c) -> p a c", c=24),
                 gcs[:, c0:c0 + n, :], ALU.subtract, reads=[t_bank[b2], t_gcs], writes=[t_edl])
            P.act(egl[:, c0:c0 + n, :], banks[b2][:, 0:n * 24].rearrange("p (a c) -> p a c", c=24), AF.Exp,
                  reads=[t_bank[b2]], writes=[t_egl])
        P.act(edl[:, 0:NC, :], edl[:, 0:NC, :], AF.Exp, reads=[t_edl], writes=[t_edl])
        P.act(eg[:, 0:NC, :], gcs[:, 0:NC, :], AF.Exp, reads=[t_gcs], writes=[t_eg])

        chk(4)
        for h in range(NH):
            wq = load_w(W[:, 1024 + h * 128:1024 + (h + 1) * 128])
            wk = load_w(W[:, 2560 + h * 128:2560 + (h + 1) * 128])
            wv = load_w(W[:, 4096 + h * 128:4096 + (h + 1) * 128])
            wz = load_w(W[:, 5632 + h * 128:5632 + (h + 1) * 128])

            def ev_raw(nt, b):
                P.copy("act", raw[:, nt * 512:(nt + 1) * 512], banks[b][:], reads=[t_bank[b]], writes=[t_raw])

            def ev_z(nt, b):
                P.act(zs[:, nt * 512:(nt + 1) * 512], banks[b][:], AF.Silu, reads=[t_bank[b]], writes=[t_zs])

            raw2 = obuf[:].rearrange("p a c -> p (a c)")

            def ev_raw2(nt, b):
                P.copy("act", raw2[:, nt * 512:(nt + 1) * 512], banks[b][:], reads=[t_bank[b]],
                       writes=[t_obuf[4 * nt + i] for i in range(4)])

            proj(T, wq, ev_raw)
            proj(T, wk, ev_raw2)
            proj(T, wz, ev_z)
            conv_silu(G, l, h, None, None, True)
            l2norm(T, "q", t_kq, 128.0 ** -0.5)
            proj(T, wv, ev_raw)
            conv_silu(G, l, 12 + h, None, None, True, raw=raw2, t_raws=[t_obuf[c] for c in range(NC)])
            l2norm(T, "k", t_kq, 1.0)
            conv_silu(G, l, 24 + h, vT[:, 0:T], t_vT, False)
            chk(5)
            for c0 in range(0, NC, 8):
                n = min(NC, c0 + 8) - c0
                b = sml.next(); bb = banks[b][:].bitcast(BF16)
                for c in range(c0, c0 + n):
                    P.tr(bb[:, (c - c0) * 128:(c - c0 + 1) * 128], kch(c), ident_b,
                         reads=[t_kT, t_cb], writes=[t_bank[b]])
                src3 = bb[:, 0:n * 128].rearrange("p (a c) -> p a c", c=128)
                for vi, (arr, t_arr, col) in enumerate(((eg, t_eg, h), (eg, t_eg, 12 + h),
                                                       (edl, t_edl, h), (edl, t_edl, 12 + h))):
                    P.tt("dve", kvar[vi][:, c0:c0 + n, :], src3,
                         arr[:, c0:c0 + n, col:col + 1].to_broadcast([128, n, 128]), ALU.mult,
                         reads=[t_bank[b], t_arr], writes=[t_kvar[vi]])
                b = sml.next(); bb = banks[b][:].bitcast(BF16)
                for c in range(c0, c0 + n):
                    P.tr(bb[:, (c - c0) * 128:(c - c0 + 1) * 128], vT[:, c * 128:(c + 1) * 128], ident_b,
                         reads=[t_vT, t_cb], writes=[t_bank[b]])
                P.copy("act", vtok[:, c0:c0 + n, :], bb[:, 0:n * 128].rearrange("p (a c) -> p a c", c=128),
                       reads=[t_bank[b]], writes=[t_vtok])

            chk(6)
            written = set()
            units = []
            for sq_ in range(nseq):
                for i in range(NCs):
                    units.append((sq_ * NCs + i, 0, i == 0, i == NCs - 1, sq_))
                    units.append((sq_ * NCs + NCs - 1 - i, 1, i == 0, i == NCs - 1, sq_))
            batches = [units[k:k + 4] for k in range(0, len(units), 4)]

            def prep_stages(bi):
                batch = batches[bi]
                par = bi % 2
                st_ = {}

                def sA1():
                    for j, (c, d, first, last, seq) in enumerate(batch):
                        hd = d * 12 + h
                        csl = slice(c * 128, (c + 1) * 128)
                        bk = sml.next()
                        st_[j] = bk
                        P.mm(banks[bk][:, 0:256], kch(c), kqf[:, c * 256:(c + 1) * 256], reads=[t_kq], writes=[t_bank[bk]])
                        P.mm(banks[bk][:, 256:384], lg[:, c, hd:hd + 1].to_broadcast([128, 128]),
                             U_f if d == 0 else L_f, reads=[t_lg, t_c32], writes=[t_bank[bk]])
                        Dm, t_Dm = U_Dm[j]
                        P.stt(Dm[:], banks[bk][:, 256:384], gcs[:, c, hd:hd + 1], msk[:, 2 * d, :],
                              ALU.subtract, ALU.add, reads=[t_bank[bk], t_gcs, t_msk], writes=[t_Dm])

                def sA2():
                    for j, (c, d, first, last, seq) in enumerate(batch):
                        Dm, t_Dm = U_Dm[j]; Es, t_Es = U_Es[j]; Ei, t_Ei = U_Ei[j]
                        P.act(Es[:], Dm[:], AF.Exp, reads=[t_Dm], writes=[t_Es])
                        P.tt("pool", Ei[:], Es[:], ident_f, ALU.add, reads=[t_Es, t_c32], writes=[t_Ei])

                def sA3():
                    for j, (c, d, first, last, seq) in enumerate(batch):
                        hd = d * 12 + h
                        bk = st_[j]
                        Es, t_Es = U_Es[j]; Ei, t_Ei = U_Ei[j]
                        Pm, t_Pm = U_P[j]; At, t_At = U_At[par][j]
                        P.stt(Pm, banks[bk][:, 0:128], nbeta[:, c, hd:hd + 1], Es[:], ALU.mult, ALU.mult,
                              reads=[t_bank[bk], t_nbeta, t_Es], writes=[t_Pm])
                        P.tt("dve", At[:], banks[bk][:, 128:256], Ei[:], ALU.mult,
                             reads=[t_bank[bk], t_Ei], writes=[t_At])

                def sB():
                    for j in range(len(batch)):
                        Pm, t_Pm = U_P[j]; PTm, t_PTm = U_PT[j]; R, t_R = U_R[j]
                        b = sml.next(); bb = banks[b][:].bitcast(BF16)
                        P.tr(bb[:, 0:128], Pm, ident_b, reads=[t_Pm, t_cb], writes=[t_bank[b]])
                        P.copy("act", PTm[:], bb[:, 0:128], reads=[t_bank[b]], writes=[t_PTm])
                        P.tt("pool", R, Pm, ident_b, ALU.add, reads=[t_Pm, t_cb], writes=[t_R])

                def mkC(k):
                    def sC():
                        for j in range(len(batch)):
                            Pm, t_Pm = U_P[j]; PTm, t_PTm = U_PT[j]; R, t_R = U_R[j]
                            b = sml.next()
                            st_[("c", j)] = b
                            if k == 0:
                                P.mm(banks[b][:, 0:128], PTm[:], Pm, reads=[t_Pm, t_PTm], writes=[t_bank[b]])
                                P.mm(banks[b][:, 256:384], Pm, PTm[:], reads=[t_Pm, t_PTm], writes=[t_bank[b]])
                            elif k < 6:
                                P.mm(banks[b][:, 0:256], PTm[:], U_PRt[j][:, 0:256], reads=[t_Pm, t_PTm, t_R],
                                     writes=[t_bank[b]])
                                P.mm(banks[b][:, 256:384], Pm, PTm[:], reads=[t_Pm, t_PTm], writes=[t_bank[b]])
                            else:
                                P.mm(banks[b][:, 128:256], PTm[:], R, reads=[t_PTm, t_R], writes=[t_bank[b]])
                        for j in range(len(batch)):
                            Pm, t_Pm = U_P[j]; PTm, t_PTm = U_PT[j]; R, t_R = U_R[j]
                            b = st_[("c", j)]
                            if k >= 1:
                                P.tt("dve", R, banks[b][:, 128:256], R, ALU.add, reads=[t_bank[b], t_R], writes=[t_R])
                            if k < 6:
                                P.copy("dve", Pm, banks[b][:, 0:128], reads=[t_bank[b]], writes=[t_Pm])
                                P.copy("act", PTm[:], banks[b][:, 256:384], reads=[t_bank[b]], writes=[t_PTm])
                    return [sC]

                def sD():
                    for j, (c, d, first, last, seq) in enumerate(batch):
                        R, t_R = U_R[j]
                        b = sml.next()
                        st_[("d", j)] = b
                        P.mm(banks[b][:, 0:128], R, vtok[:, c, :], reads=[t_R, t_vtok], writes=[t_bank[b]])
                        P.mm(banks[b][:, 128:256], kvar[d][:, c, :], R, reads=[t_R, t_kvar[d]], writes=[t_bank[b]])
                    for j, (c, d, first, last, seq) in enumerate(batch):
                        hd = d * 12 + h
                        b = st_[("d", j)]
                        upb, t_upb = U_upb[par][j]; wpT, t_wpT = U_wpT[par][j]
                        P.act(upb[:], banks[b][:, 0:128], AF.Identity, scale=nbeta[:, c, hd:hd + 1],
                              reads=[t_bank[b], t_nbeta], writes=[t_upb])
                        P.copy("dve", wpT[:], banks[b][:, 128:256], reads=[t_bank[b]], writes=[t_wpT])

                stages = [sA1, sA2, sA3, sB]
                for lev in range(0, 7):
                    stages += mkC(lev)
                stages.append(sD)
                return stages

            def rec_step(bi, j):
                c, d, first, last, seq = batches[bi][j]
                par = bi % 2
                hd = d * 12 + h
                csl = slice(c * 128, (c + 1) * 128)
                At, t_At = U_At[par][j]; upb, t_upb = U_upb[par][j]; wpT, t_wpT = U_wpT[par][j]
                if first:
                    if grid:
                        P.dma("sp", S[:, hd, :], s0_in[l, d, h], writes=[t_S[hd]])
                        P.copy("act", Sb[:, hd, :], S[:, hd, :], reads=[t_S[hd]], writes=[t_Sb[hd]])
                    else:
                        P.memset("pool", S[:, hd, :], 0.0, writes=[t_S[hd]])
                        P.memset("pool", Sb[:, hd, :], 0.0, writes=[t_Sb[hd]])
                b = sml.next()
                P.mm(banks[b][:, 0:128], wpT[:], Sb[:, hd, :], reads=[t_wpT, t_Sb[hd]], writes=[t_bank[b]])
                P.mm(banks[b][:, 128:256], qch(c), Sb[:, hd, :], reads=[t_qT, t_Sb[hd]], writes=[t_bank[b]])
                vn, t_vn = r_vn.next(); o1, t_o1 = r_o1.next()
                P.stt(vn[:], banks[b][:, 0:128], nbeta[:, c, hd:hd + 1], upb[:], ALU.mult, ALU.subtract,
                      reads=[t_bank[b], t_nbeta, t_upb], writes=[t_vn])
                P.act(o1[:], banks[b][:, 128:256], AF.Identity, scale=eg[:, c, hd:hd + 1],
                      reads=[t_bank[b], t_eg], writes=[t_o1])
                b2 = sml.next()
                P.mm(banks[b2][:, 0:128], At[:], vn[:], reads=[t_At, t_vn], writes=[t_bank[b2]])
                P.mm(banks[b2][:, 128:256], kvar[2 + d][:, c, :], vn[:], reads=[t_kvar[2 + d], t_vn],
                     writes=[t_bank[b2]])
                P.stt(S[:, hd, :], S[:, hd, :], egl[:, c, hd:hd + 1], banks[b2][:, 128:256], ALU.mult, ALU.add,
                      reads=[t_S[hd], t_egl, t_bank[b2]], writes=[t_S[hd]])
                if not last:
                    P.copy("act", Sb[:, hd, :], S[:, hd, :], reads=[t_S[hd]], writes=[t_Sb[hd]])
                if c in written:
                    P.tt("dve", o1[:], banks[b2][:, 0:128], o1[:], ALU.add, reads=[t_bank[b2], t_o1], writes=[t_o1])
                    P.tt("pool", obuf[:, c, :], obuf[:, c, :], o1[:], ALU.add, reads=[t_o1, t_obuf[c]],
                         writes=[t_obuf[c]])
                else:
                    P.tt("dve", obuf[:, c, :], banks[b2][:, 0:128], o1[:], ALU.add,
                         reads=[t_bank[b2], t_o1], writes=[t_obuf[c]])
                    written.add(c)
                if last and not grid:
                    P.dma("sp", st_out[seq, l, d, h], S[:, hd, :], reads=[t_S[hd]])

            for stg in prep_stages(0):
                stg()
            for bi in range(len(batches)):
                nxt = prep_stages(bi + 1) if bi + 1 < len(batches) else []
                nrec = len(batches[bi])
                per = (len(nxt) + nrec - 1) // nrec if nxt else 0
                k = 0
                for j in range(nrec):
                    rec_step(bi, j)
                    for _ in range(per):
                        if k < len(nxt):
                            nxt[k](); k += 1
                while k < len(nxt):
                    nxt[k](); k += 1

            allo = [t_obuf[c] for c in range(NC)]
            for c0 in range(0, NC, 4):
                n = min(NC, c0 + 4) - c0
                st, t_st = r_st.next()
                st3 = st[:, 0:n * 128].rearrange("p (a c) -> p a c", c=128)
                P.tt("pool", st3, obuf[:, c0:c0 + n, :], obuf[:, c0:c0 + n, :], ALU.mult,
                     reads=allo[c0:c0 + n], writes=[t_st])
                P.reduce_sum(ssq[:, c0:c0 + n], st3, reads=[t_st], writes=[t_ssq])
            P.act(ssq[:, 0:NC], ssq[:, 0:NC], AF.Sqrt, bias=EPS, scale=1.0 / 128, reads=[t_ssq], writes=[t_ssq])
            P.recip(ssq[:, 0:NC], ssq[:, 0:NC], reads=[t_ssq], writes=[t_ssq])
            on3 = scrB[:, 0:T].rearrange("p (a c) -> p a c", c=128)
            P.tt("dve", on3, obuf[:, 0:NC, :], ssq[:, 0:NC].unsqueeze(2).to_broadcast([128, NC, 128]), ALU.mult,
                 reads=allo + [t_ssq], writes=[t_scrB])
            for c0 in range(0, NC, 8):
                n = min(NC, c0 + 8) - c0
                b = sml.next(); bb = banks[b][:].bitcast(BF16)
                for c in range(c0, c0 + n):
                    P.tr(bb[:, (c - c0) * 128:(c - c0 + 1) * 128], on3[:, c, :], ident_b,
                         reads=[t_scrB, t_cb], writes=[t_bank[b]])
                P.stt(zs[:, c0 * 128:(c0 + n) * 128], bb[:, 0:n * 128], smT[:, R_GN + l:R_GN + l + 1],
                      zs[:, c0 * 128:(c0 + n) * 128], ALU.mult, ALU.mult,
                      reads=[t_bank[b], t_smT, t_zs], writes=[t_zs])
            P.dma("sp", yscr[4 + h, :, 0:T], zs[:, 0:T], reads=[t_zs], writes=[t_yscr[4 + h]])

        chk(8)
        for fg in range(4):
            wu = load_w(W[:, fg * 128:(fg + 1) * 128])
            wzf = load_w(W[:, 512 + fg * 128:512 + (fg + 1) * 128])

            def ev_u(nt, b):
                P.copy("act", qT[:, nt * 512:(nt + 1) * 512], banks[b][:], reads=[t_bank[b]], writes=[t_qT])

            def ev_zf(nt, b):
                P.act(zs[:, nt * 512:(nt + 1) * 512], banks[b][:], AF.Silu, reads=[t_bank[b]], writes=[t_zs])

            proj(T, wu, ev_u)
            proj(T, wzf, ev_zf)
            if not grid:
                fscale = float((L * 128) ** -0.5)
                for s in range(nseq):
                    b = sml.next(); bb = banks[b][:].bitcast(BF16)
                    for j in range(2):
                        c = s * 2 + j
                        P.tr(bb[:, j * 128:(j + 1) * 128], qT[:, c * 128:(c + 1) * 128], ident_b,
                             reads=[t_qT, t_cb], writes=[t_bank[b]])
                    P.copy("act", kT[:, 0:256], bb[:, 0:256], reads=[t_bank[b]], writes=[t_kT])
                    b = sml.next()
                    for j in range(2):
                        P.mm(banks[b][:], kT[:, j * 128:(j + 1) * 128], dp[:, j, :], start=(j == 0), stop=(j == 1),
                             reads=[t_kT, t_dp], writes=[t_bank[b]])
                    P.copy("dve", vT[:, 0:512], banks[b][:], reads=[t_bank[b]], writes=[t_vT])
                    b = sml.next()
                    P.mm(banks[b][:, 0:256], dc[:, 0, :], vT[:, 0:256], start=True, stop=False,
                         reads=[t_dc, t_vT], writes=[t_bank[b]])
                    P.mm(banks[b][:, 0:256], dc[:, 1, :], vT[:, 256:512], start=False, stop=True,
                         reads=[t_dc, t_vT], writes=[t_bank[b]])
                    P.stt(zs[:, s * 256:(s + 1) * 256], banks[b][:, 0:256], fscale, zs[:, s * 256:(s + 1) * 256],
                          ALU.mult, ALU.mult, reads=[t_bank[b], t_zs], writes=[t_zs])
            else:
                fscale = float((2048 * 128) ** -0.5)
                u4 = qT[:, 0:2048].rearrange("p (r w) -> p w r", w=64)
                zre = kvar[0][:].rearrange("p a c -> p (a c)"); zim = kvar[1][:].rearrange("p a c -> p (a c)")
                z2re = kvar[2][:].rearrange("p a c -> p (a c)"); z2im = kvar[3][:].rearrange("p a c -> p (a c)")
                P.copy("dve", vT[:, 0:2048].rearrange("p (w r) -> p w r", w=64), u4, reads=[t_qT], writes=[t_vT])
                for j in range(16):
                    b = sml.next(); bb = banks[b][:].bitcast(BF16)
                    P.tr(bb[:, 0:128], vT[:, j * 128:(j + 1) * 128], ident_b,
                         reads=[t_vT, t_cb], writes=[t_bank[b]])
                    tk, t_tk = r_P.next()
                    P.copy("act", tk[:, 0:128], bb[:, 0:128], reads=[t_bank[b]], writes=[t_tk])
                    b = sml.next()
                    P.mm(banks[b][:, 0:256], tk[:, 0:128], dsm[:, 0, :], reads=[t_tk, t_dsm], writes=[t_bank[b]])
                    P.copy("dve", zre[:, j * 128:(j + 1) * 128], banks[b][:, 0:128], reads=[t_bank[b]], writes=[t_kvar[0]])
                    P.copy("act", zim[:, j * 128:(j + 1) * 128], banks[b][:, 128:256], reads=[t_bank[b]], writes=[t_kvar[1]])
                zre3 = zre.rearrange("p (w k) -> p k w", k=32); zim3 = zim.rearrange("p (w k) -> p k w", k=32)
                P.copy("dve", kT[:, 0:2048].rearrange("p (k w) -> p k w", k=32), zre3, reads=[t_kvar[0]], writes=[t_kT])
                P.copy("pool", scrB[:, 0:2048].rearrange("p (k w) -> p k w", k=32), zim3, reads=[t_kvar[1]], writes=[t_scrB])
                for j in range(16):
                    b = sml.next(); bb = banks[b][:].bitcast(BF16)
                    P.tr(bb[:, 0:128], kT[:, j * 128:(j + 1) * 128], ident_b,
                         reads=[t_kT, t_cb], writes=[t_bank[b]])
                    P.tr(bb[:, 128:256], scrB[:, j * 128:(j + 1) * 128], ident_b,
                         reads=[t_scrB, t_cb], writes=[t_bank[b]])
                    tk, t_tk = r_P.next(); tk2, t_tk2 = r_PT.next()
                    P.copy("act", tk[:, 0:128], bb[:, 0:128], reads=[t_bank[b]], writes=[t_tk])
                    P.copy("dve", tk2[:], bb[:, 128:256], reads=[t_bank[b]], writes=[t_tk2])
                    b = sml.next()
                    P.mm(banks[b][:, 0:256], tk[:, 0:128], dsm[:, 1, :], start=True, stop=False,
                         reads=[t_tk, t_dsm], writes=[t_bank[b]])
                    P.mm(banks[b][:, 0:256], tk2[:], dsm[:, 2, :], start=False, stop=True,
                         reads=[t_tk2, t_dsm], writes=[t_bank[b]])
                    P.copy("dve", z2re[:, j * 128:(j + 1) * 128], banks[b][:, 0:128], reads=[t_bank[b]], writes=[t_kvar[2]])
                    P.copy("act", z2im[:, j * 128:(j + 1) * 128], banks[b][:, 128:256], reads=[t_bank[b]], writes=[t_kvar[3]])
                for nt in range(4):
                    tsl = slice(nt * 512, (nt + 1) * 512)
                    b = big.next()
                    P.mm(banks[b][:], dc[:, 0, :], z2re[:, tsl], start=True, stop=False,
                         reads=[t_dc, t_kvar[2]], writes=[t_bank[b]])
                    P.mm(banks[b][:], dc[:, 1, :], z2im[:, tsl], start=False, stop=True,
                         reads=[t_dc, t_kvar[3]], writes=[t_bank[b]])
                    P.stt(zs[:, tsl], banks[b][:], fscale, zs[:, tsl], ALU.mult, ALU.mult,
                          reads=[t_bank[b], t_zs], writes=[t_zs])
            P.dma("sp", yscr[fg, :, 0:T], zs[:, 0:T], reads=[t_zs], writes=[t_yscr[fg]])

        chk(9)
        for kc in range(KC):
            for nt in range(NT):
                P.dma("sp", hT[:, kc, nt * 512:(nt + 1) * 512], yscr[kc, :, nt * 512:(nt + 1) * 512],
                      reads=[t_yscr[kc]], writes=[], joins=[t_hT[nt]])
        for j in range(KC):
            wi = load_w(w_out[l][:, j * 128:(j + 1) * 128])

            def ev_o(nt, b, j=j):
                tsl = slice(nt * 512, (nt + 1) * 512)
                st, t_st = r_st.next()
                P.dma("sp", st[:], src[j, :, tsl], reads=[t_src[j][nt]], writes=[t_st])
                P.stt(st[:], banks[b][:], modT[:, l, 32 + j, gi:gi + 1], st[:], ALU.mult, ALU.add,
                      reads=[t_bank[b], t_modT, t_st], writes=[t_st])
                P.dma("sp", dst[j, :, tsl], st[:], reads=[t_st], writes=[t_dst[j][nt]])

            proj(T, wi, ev_o)

    t_yscr = [Trk() for _ in range(KC)]
    try:
        for g in groups:
            NT = GROUPS[g]["T"] // 512
            if g == "p":
                P.dma("sp", dp, dftp, writes=[t_dp])
            else:
                P.dma("sp", dsm, dfts, writes=[t_dsm])
            t_in = [[Trk() for _ in range(NT)] for _ in range(KC)]
            t_sc = [[Trk() for _ in range(NT)] for _ in range(KC)]
            for l in range(depth):
                src, t_src = (xin[g], t_in) if l == 0 else (xscr[g], t_sc)
                layer(g, l, src, t_src, xscr[g], t_sc)
                chk(10)
            norm_pass(g, xscr[g], t_sc, 0, True)
    except _Stop:
        pass
    P.build()
    return nc


def _consts():
    bf = ml_dtypes.bfloat16
    i = np.arange(128)
    ident = np.eye(128, dtype=np.float32)
    U = (i[:, None] <= i[None, :]).astype(np.float32)
    Lw = (i[:, None] >= i[None, :]).astype(np.float32)
    ones = np.ones((128, 128), np.float32)
    cf32 = np.stack([ident, U, Lw, ones], axis=1)
    NEG = -30000.0
    negf = np.where(i[:, None] < i[None, :], 0.0, NEG)
    negb = np.where(i[:, None] > i[None, :], 0.0, NEG)
    masks = np.stack([negf, negf, negb, negb], axis=1).astype(np.float32)
    cbf = np.stack([ident, ones], axis=1).astype(bf)
    ang = 2 * np.pi * np.outer(i, i) / 128.0
    dftc = np.stack([np.cos(ang), np.sin(ang)], axis=1).astype(bf)
    k = np.arange(256)
    dftp = np.zeros((128, 2, 512), np.float64)
    for j in range(2):
        a = 2 * np.pi * np.outer(j * 128 + i, k) / 256.0
        dftp[:, j, 0:256] = np.cos(a)
        dftp[:, j, 256:512] = -np.sin(a)
    dftp = dftp.astype(bf)
    r = np.arange(32)
    a32 = 2 * np.pi * np.outer(r, r) / 32.0
    bd32c = np.kron(np.eye(4), np.cos(a32)); bd32s = np.kron(np.eye(4), np.sin(a32))
    w = np.arange(64)
    a64 = 2 * np.pi * np.outer(w, w) / 64.0
    bd64c = np.kron(np.eye(2), np.cos(a64)); bd64s = np.kron(np.eye(2), np.sin(a64))
    dfts = np.stack([np.concatenate([bd32c, -bd32s], 1), np.concatenate([bd64c, -bd64s], 1),
                     np.concatenate([bd64s, bd64c], 1)], axis=1).astype(bf)
    return dict(cf32=cf32, masks=masks, cbf=cbf, dftc=dftc, dftp=dftp, dfts=dfts)


def _core_inputs(i, depth, groups, x_prompt, x_sample, state_ctx, c, c_ctx, norm_w, w_mod, b_mod, w_in,
                 conv_w, a_log, dt_bias, gnorm_w, w_out, final_norm_w, consts):
    f = np.float32
    m = dict(consts)
    b = i % x_sample.shape[0]
    if "p" in groups:
        xp = np.asarray(x_prompt[2 * i:2 * i + 2], f).reshape(512, D)
        m["xT_p"] = np.ascontiguousarray(xp.T).reshape(KC, 128, 512)
    if "s" in groups:
        m["xT_s"] = np.ascontiguousarray(np.asarray(x_sample[b], f).T).reshape(KC, 128, 2048)
        m["s0"] = np.ascontiguousarray(state_ctx[b][:depth], dtype=f)
    m["w_in"] = w_in; m["w_out"] = w_out; m["w_mod"] = w_mod
    small = np.zeros((768, 128), f)
    for l in range(depth):
        small[R_NORM + l * 16:R_NORM + (l + 1) * 16] = norm_w[l].reshape(16, 128)
        small[R_BMOD + l * 48:R_BMOD + (l + 1) * 48] = b_mod[l].reshape(48, 128)
        small[R_CONV + l * 108:R_CONV + (l + 1) * 108] = conv_w[l].reshape(108, 128)
        small[R_GN + l] = gnorm_w[l]
    small[R_FN:R_FN + 16] = final_norm_w.reshape(16, 128)
    small[R_CCTX:R_CCTX + 16] = c_ctx.reshape(16, 128)
    small[R_C:R_C + 16] = c[b].reshape(16, 128)
    m["small"] = small
    al = np.zeros(192, f)
    al[0:depth * 24] = a_log[:depth].reshape(-1)
    al[96:96 + depth * 24] = dt_bias[:depth].reshape(-1)
    m["albt"] = np.ascontiguousarray(np.broadcast_to(al[None, :], (128, 192)))
    return m


_NC_CACHE = {}


def run_cores(inputs, depth=4, groups=("p", "s"), n_cores=8, runner=None):
    key = (depth, tuple(groups))
    if key not in _NC_CACHE:
        _NC_CACHE[key] = build_program(depth, groups)
    nc = _NC_CACHE[key]
    consts = _consts()
    a = {k: np.asarray(v) for k, v in inputs.items()}
    w_in = np.ascontiguousarray(a["w_in"][:depth], dtype=np.float32)
    w_out = np.ascontiguousarray(a["w_out"][:depth], dtype=np.float32)
    w_mod = np.ascontiguousarray(a["w_mod"][:depth], dtype=np.float32)
    in_maps = [_core_inputs(i, depth, groups, a["x_prompt"], a["x_sample"], a["state_ctx"], a["c"], a["c_ctx"],
                            a["norm_w"], w_mod, a["b_mod"], w_in, a["conv_w"], a["a_log"], a["dt_bias"],
                            a["gnorm_w"], w_out, a["final_norm_w"], consts) for i in range(n_cores)]
    if runner is None:
        res = run_bass_kernel_spmd(nc, in_maps, core_ids=list(range(n_cores))).results
    else:
        res = runner(nc, in_maps)
    return res


def kernel(x_prompt, x_sample, state_ctx, c, c_ctx, norm_w, w_mod, b_mod, w_in,
           conv_w, a_log, dt_bias, gnorm_w, w_out, final_norm_w):
    inputs = dict(x_prompt=x_prompt, x_sample=x_sample, state_ctx=state_ctx, c=c, c_ctx=c_ctx, norm_w=norm_w,
                  w_mod=w_mod, b_mod=b_mod, w_in=w_in, conv_w=conv_w, a_log=a_log, dt_bias=dt_bias,
                  gnorm_w=gnorm_w, w_out=w_out, final_norm_w=final_norm_w)
    res = run_cores(inputs)
    B = np.asarray(x_prompt).shape[0]
    BS = np.asarray(x_sample).shape[0]
    y_prompt = np.zeros((B, 256, D), np.float32)
    y_sample = np.zeros((BS, 2048, D), np.float32)
    state_new = np.zeros((B, 4, 2, NH, 128, 128), np.float32)
    for i in range(8):
        r = res[i]
        y_prompt[2 * i:2 * i + 2] = np.asarray(r["yT_p"]).reshape(D, 512).T.reshape(2, 256, D)
        state_new[2 * i:2 * i + 2] = np.asarray(r["st"])
        if i < BS:
            y_sample[i] = np.asarray(r["yT_s"]).reshape(D, 2048).T
    return (y_prompt, y_sample, state_new)
```

```python
import numpy as np
import ml_dtypes
from contextlib import ExitStack
import concourse.bass as bass
import concourse.mybir as mybir
from concourse.bass_utils import run_bass_kernel_spmd

F32 = mybir.dt.float32
BF16 = mybir.dt.bfloat16
AF = mybir.ActivationFunctionType
ALU = mybir.AluOpType
AX = mybir.AxisListType

SAME_ENGINE_WAITS = True
N_DMA_SEMS = 40
N_SW_SEMS = 8


class Trk:
    __slots__ = ("w", "r", "name", "excl")

    def __init__(self, name="", excl=False):
        self.w = {}
        self.r = {}
        self.name = name
        self.excl = excl


class Prog:
    ENGS = ("pe", "act", "dve", "pool", "sp")

    def __init__(self, nc):
        self.nc = nc
        self.ops = {e: [] for e in self.ENGS}
        self.sem = {e: nc.alloc_semaphore("sem_" + e) for e in self.ENGS if e != "sp"}
        self.cnt = {e: 0 for e in self.ENGS}
        self.known = {e: {} for e in self.ENGS}
        self.dsem = [nc.alloc_semaphore("dsem%d" % i) for i in range(N_DMA_SEMS)]
        self.dcnt = [0] * N_DMA_SEMS
        self.drr = 0
        self.drr_sw = 0
        self.stack = ExitStack()
        self.nalloc = 0

    def sb(self, shape, dtype, name=None):
        self.nalloc += 1
        name = name or ("sb%d" % self.nalloc)
        t = self.stack.enter_context(self.nc.sbuf_tensor(name, list(shape), dtype))
        return t

    def ps(self, shape, dtype, name=None):
        self.nalloc += 1
        name = name or ("ps%d" % self.nalloc)
        t = self.stack.enter_context(self.nc.psum_tensor(name, list(shape), dtype))
        return t

    def _need(self, e, reads, writes, joins=()):
        need = {}
        for t in reads:
            for k, v in t.w.items():
                if need.get(k, (None, 0))[1] < v[1]:
                    need[k] = v
            if t.excl:
                for k, v in t.r.items():
                    if k != e and need.get(k, (None, 0))[1] < v[1]:
                        need[k] = v
        for t in writes:
            for d in (t.w, t.r):
                for k, v in d.items():
                    if need.get(k, (None, 0))[1] < v[1]:
                        need[k] = v
        for t in joins:
            for k, v in t.r.items():
                if need.get(k, (None, 0))[1] < v[1]:
                    need[k] = v
        kn = self.known[e]
        waits = []
        for k, (sem, val) in need.items():
            if kn.get(k, 0) >= val:
                continue
            if k == e and (not SAME_ENGINE_WAITS or e == "pe" or val > self.cnt[e]):
                continue
            kn[k] = val
            waits.append((sem, val))
        return waits

    def emit(self, e, fn, reads=(), writes=(), inc=True):
        waits = self._need(e, reads, writes)
        val = self.cnt[e] + 1
        if inc:
            self.cnt[e] = val
        ev = (self.sem[e], val)
        self.ops[e].append((fn, waits, inc))
        for t in reads:
            t.r[e] = ev
        for t in writes:
            t.w = {e: ev}
            t.r = {}

    def dma(self, q, out_ap, in_ap, reads=(), writes=(), joins=()):
        if q == "pool":
            i = N_DMA_SEMS - N_SW_SEMS + self.drr_sw
            self.drr_sw = (self.drr_sw + 1) % N_SW_SEMS
        else:
            i = self.drr
            self.drr = (i + 1) % (N_DMA_SEMS - N_SW_SEMS)
        sem = self.dsem[i]
        key = ("d", i)
        waits = self._need(q, reads, writes, joins)
        prev = self.dcnt[i]
        if prev > 0 and self.known[q].get(key, 0) < prev:
            self.known[q][key] = prev
            waits.append((sem, prev))
        self.dcnt[i] = prev + 16
        ev = (sem, prev + 16)

        def fn(eng, out_ap=out_ap, in_ap=in_ap, sem=sem):
            return eng.dma_start(out=out_ap, in_=in_ap).then_inc(sem, 16)

        self.ops[q].append((fn, waits, False))
        for t in reads:
            t.r[key] = ev
        for t in writes:
            t.w = {key: ev}
            t.r = {}
        for t in joins:
            t.w[key] = ev
            t.r = {}

    def mm(self, out, lhsT, rhs, start=True, stop=True, reads=(), writes=(), inc=None):
        if inc is None:
            inc = stop
        self.emit("pe", lambda eng: eng.matmul(out, lhsT, rhs, start=start, stop=stop),
                  reads, writes, inc)

    def tr(self, out, in_, ident, reads=(), writes=(), inc=True):
        self.emit("pe", lambda eng: eng.transpose(out, in_, ident), reads, writes, inc)

    def act(self, out, in_, func, bias=0.0, scale=1.0, reads=(), writes=()):
        self.emit("act", lambda eng: eng.activation(out, in_, func, bias=bias, scale=scale),
                  reads, writes)

    def tt(self, e, out, in0, in1, op, reads=(), writes=()):
        self.emit(e, lambda eng: eng.tensor_tensor(out, in0, in1, op), reads, writes)

    def ts(self, e, out, in0, s1, s2, op0, op1=None, reads=(), writes=()):
        if op1 is None:
            self.emit(e, lambda eng: eng.tensor_scalar(out, in0, s1, None, op0), reads, writes)
        else:
            self.emit(e, lambda eng: eng.tensor_scalar(out, in0, s1, s2, op0, op1), reads, writes)

    def stt(self, out, in0, scalar, in1, op0, op1, reads=(), writes=()):
        self.emit("dve", lambda eng: eng.scalar_tensor_tensor(out, in0, scalar, in1, op0, op1),
                  reads, writes)

    def copy(self, e, out, in_, reads=(), writes=()):
        import os
        if e == "act":
            if os.environ.get("ALT_ACT"):
                self.emit("act", lambda eng: eng.activation(out, in_, AF.Identity), reads, writes)
            else:
                self.emit("act", lambda eng: eng.copy(out, in_), reads, writes)
        else:
            if os.environ.get("ALT_DVE") and e == "dve":
                self.emit(e, lambda eng: eng.tensor_scalar(out, in_, 1.0, None, ALU.mult), reads, writes)
            else:
                self.emit(e, lambda eng: eng.tensor_copy(out, in_), reads, writes)

    def memset(self, e, ap, val, writes=()):
        self.emit(e, lambda eng: eng.memset(ap, val), (), writes)

    def recip(self, out, in_, reads=(), writes=()):
        self.emit("dve", lambda eng: eng.reciprocal(out, in_), reads, writes)

    def reduce_sum(self, out, in_, reads=(), writes=()):
        self.emit("dve", lambda eng: eng.reduce_sum(out, in_, AX.X), reads, writes)

    def finish(self):
        fw = []
        for i in range(N_DMA_SEMS):
            if self.dcnt[i] > 0:
                fw.append((self.dsem[i], self.dcnt[i]))
        for e in ("pe", "act", "dve", "pool"):
            if self.cnt[e] > 0:
                fw.append((self.sem[e], self.cnt[e]))
        self.final_waits = fw

    def build(self):
        nc = self.nc
        self.finish()
        prog = self

        def replay(e, eng):
            for fn, waits, inc in prog.ops[e]:
                for sem, val in waits:
                    eng.wait_ge(sem, val)
                ins = fn(eng)
                if inc:
                    ins.then_inc(prog.sem[e], 1)

        with nc.Block() as block:
            @block.tensor
            def _(eng):
                replay("pe", eng)

            @block.scalar
            def _(eng):
                replay("act", eng)

            @block.vector
            def _(eng):
                replay("dve", eng)

            @block.gpsimd
            def _(eng):
                replay("pool", eng)

            @block.sync
            def _(eng):
                replay("sp", eng)
                for sem, val in prog.final_waits:
                    eng.wait_ge(sem, val)
        self.stack.close()


D = 2048
KC = 16
NH = 12
DIN = 7216
EPS = 1e-6
GROUPS = {
    "p": dict(T=512, nseq=2, L=256, grid=False),
    "s": dict(T=2048, nseq=1, L=2048, grid=True),
}
TMAX = 2048
NCMAX = 16
R_NORM, R_BMOD, R_CONV, R_GN, R_FN, R_CCTX, R_C = 0, 64, 256, 688, 692, 708, 724


class Rot:
    def __init__(self, items):
        self.items = items
        self.i = 0

    def next(self):
        it = self.items[self.i]
        self.i = (self.i + 1) % len(self.items)
        return it


class _Stop(Exception):
    pass


def build_program(depth=4, groups=("p", "s")):
    import os
    nc = bass.Bass("TRN2", target_bir_lowering=False)
    P = Prog(nc)
    KSTOP = int(os.environ.get("KSTOP", "0"))

    def chk(n):
        if KSTOP == n:
            raise _Stop()

    def din(name, shape, dt=F32):
        return nc.dram_tensor(name, list(shape), dt, kind="ExternalInput").ap()

    def dout(name, shape, dt=F32):
        return nc.dram_tensor(name, list(shape), dt, kind="ExternalOutput").ap()

    def dscr(name, shape, dt=F32):
        return nc.dram_tensor(name, list(shape), dt, kind="Internal").ap()

    xin = {g: din("xT_" + g, [KC, 128, GROUPS[g]["T"]]) for g in groups}
    yout = {g: dout("yT_" + g, [KC, 128, GROUPS[g]["T"]]) for g in groups}
    xscr = {g: dscr("xscr_" + g, [KC, 128, GROUPS[g]["T"]]) for g in groups}
    yscr = dscr("yscr", [KC, 128, TMAX], BF16)
    w_in = din("w_in", [depth, D, DIN])
    w_out = din("w_out", [depth, D, D])
    w_mod = din("w_mod", [depth, D, 3 * D])
    small = din("small", [768, 128])
    albt = din("albt", [128, 192])
    cf32 = din("cf32", [128, 4, 128])
    masks = din("masks", [128, 4, 128])
    cbf = din("cbf", [128, 2, 128], BF16)
    dftc = din("dftc", [128, 2, 128], BF16)
    dftp = din("dftp", [128, 2, 512], BF16)
    dfts = din("dfts", [128, 3, 256], BF16)
    if "s" in groups:
        s0_in = din("s0", [depth, 2, NH, 128, 128])
    if "p" in groups:
        st_out = dout("st", [2, depth, 2, NH, 128, 128])

    hT = P.sb([128, KC, TMAX], BF16, "hT")
    t_hT = [Trk() for _ in range(4)]
    NW = 4
    wring = [P.sb([128, KC, 128], BF16, "wr%d" % i) for i in range(NW)]
    t_wring = [Trk() for _ in range(NW)]
    wrot = Rot(list(range(NW)))
    raw = P.sb([128, TMAX], F32, "raw"); t_raw = Trk()
    cv = P.sb([128, TMAX], F32, "cv"); t_cv = Trk()
    abt = raw[:, 0:NCMAX * 48].rearrange("p (a c) -> p a c", c=48); t_abt = t_raw
    kqf = P.sb([128, 2 * TMAX], BF16, "kqf"); t_kq = Trk()
    t_qT = t_kq; t_kT = t_kq
    qT = kqf[:, 0:TMAX]
    kT = kqf[:, TMAX:2 * TMAX]

    def kch(c):
        return kqf[:, c * 256:c * 256 + 128]

    def qch(c):
        return kqf[:, c * 256 + 128:(c + 1) * 256]
    vT = P.sb([128, TMAX], BF16, "vT"); t_vT = Trk()
    zs = P.sb([128, TMAX], BF16, "zs"); t_zs = Trk()
    kvar = [P.sb([128, NCMAX, 128], BF16, "kvar%d" % i) for i in range(4)]
    t_kvar = [Trk() for _ in range(4)]
    vtok = P.sb([128, NCMAX, 128], BF16, "vtok"); t_vtok = Trk()
    obuf = P.sb([128, NCMAX, 128], F32, "obuf"); t_obuf = [Trk() for _ in range(NCMAX)]
    scrB = P.sb([128, TMAX], BF16, "scrB"); t_scrB = Trk()
    rstd = scrB[:, 0:1024].bitcast(F32); t_rstd = t_scrB
    S = P.sb([128, 24, 128], F32, "S"); Sb = P.sb([128, 24, 128], BF16, "Sb")
    t_S = [Trk() for _ in range(24)]; t_Sb = [Trk() for _ in range(24)]
    lg = P.sb([128, NCMAX, 24], F32, "lg"); t_lg = Trk()
    nbeta = P.sb([128, NCMAX, 24], F32, "nbeta"); t_nbeta = Trk()
    gcs = P.sb([128, NCMAX, 24], F32, "gcs"); t_gcs = Trk()
    eg = P.sb([128, NCMAX, 24], F32, "eg"); t_eg = Trk()
    edl = P.sb([128, NCMAX, 24], F32, "edl"); t_edl = Trk()
    egl = P.sb([128, NCMAX, 24], F32, "egl"); t_egl = Trk()
    sqs = [P.sb([128, 512], BF16, "sq%d" % i) for i in range(2)]; t_sqs = [Trk(), Trk()]
    sqrot = Rot([0, 1])
    smT = P.sb([128, 768], F32, "smT"); t_smT = Trk()
    albc = P.sb([128, 192], F32, "albc"); t_albc = Trk()
    nea = P.sb([128, 96], F32, "nea"); t_nea = Trk()
    scT = P.sb([128, KC, 2], BF16, "scT"); t_scT = Trk()
    modT = P.sb([128, depth, 48, 2], F32, "modT"); t_modT = Trk()
    Amod = P.sb([128, depth, 2, KC], F32, "Amod"); t_Amod = Trk()
    c32 = P.sb([128, 4, 128], F32, "c32"); t_c32 = Trk()
    msk = P.sb([128, 4, 128], F32, "msk"); t_msk = Trk()
    cb = P.sb([128, 2, 128], BF16, "cb"); t_cb = Trk()
    dc = P.sb([128, 2, 128], BF16, "dc"); t_dc = Trk()
    dpos = P.sb([128, 1024], BF16, "dpos"); t_dp = Trk(); t_dsm = t_dp
    dp = dpos[:, 0:1024].rearrange("p (a c) -> p a c", c=512)
    dsm = dpos[:, 0:768].rearrange("p (a c) -> p a c", c=256)
    def mk(n, dt, name, k=2):
        return Rot([(P.sb([128, n], dt, "%s%d" % (name, i)), Trk()) for i in range(k)])
    def mkU(n, dt, name, k=4):
        return [(P.sb([128, n], dt, "%s%d" % (name, i)), Trk()) for i in range(k)]
    U_Dm = mkU(128, F32, "uDm"); U_Es = mkU(128, F32, "uEs"); U_Ei = mkU(128, F32, "uEi")
    U_PRt = [P.sb([128, 256], BF16, "uPR%d" % i) for i in range(4)]
    U_P = [(U_PRt[i][:, 0:128], Trk()) for i in range(4)]
    U_R = [(U_PRt[i][:, 128:256], Trk()) for i in range(4)]
    U_PT = mkU(128, BF16, "uPT")
    U_At = [mkU(128, BF16, "uAt%d" % p) for p in range(2)]
    U_upb = [mkU(128, BF16, "uupb%d" % p) for p in range(2)]
    U_wpT = [mkU(128, BF16, "uwpT%d" % p) for p in range(2)]
    r_P = Rot([(U_PRt[i], U_P[i][1]) for i in range(4)]); r_PT = Rot(U_PT)
    r_vn = mk(128, BF16, "vn"); r_o1 = mk(128, F32, "o1")
    r_st = mk(512, F32, "stg", 2)
    ssq = P.sb([128, NCMAX], F32, "ssq"); t_ssq = Trk()

    banks = [P.ps([128, 512], F32, "bank%d" % i) for i in range(8)]
    t_bank = [Trk(excl=True) for _ in range(8)]
    big = Rot([0, 1, 2, 3])
    sml = Rot([4, 5, 6, 7, 0, 1, 2, 3])

    ident_f = c32[:, 0, :]
    U_f = c32[:, 1, :]
    L_f = c32[:, 2, :]
    ones_f = c32[:, 3, :]
    ident_b = cb[:, 0, :]
    ones_b = cb[:, 1, :]

    P.dma("sp", c32[:], cf32, writes=[t_c32])
    P.dma("sp", msk[:], masks, writes=[t_msk])
    P.dma("sp", cb[:], cbf, writes=[t_cb])
    P.dma("sp", dc[:], dftc, writes=[t_dc])
    P.dma("sp", albc[:], albt, writes=[t_albc])
    stg6 = cv[:, 0:768].rearrange("p (a c) -> p a c", a=6)
    P.dma("sp", stg6, small.rearrange("(a p) c -> p a c", p=128), writes=[t_cv])
    for a in range(6):
        b = sml.next()
        P.tr(banks[b][:, 0:128], stg6[:, a, :], ident_f, reads=[t_cv, t_c32], writes=[t_bank[b]])
        P.copy("dve", smT[:, a * 128:(a + 1) * 128], banks[b][:, 0:128], reads=[t_bank[b]], writes=[t_smT])
    P.act(nea[:], albc[:, 0:96], AF.Exp, reads=[t_albc], writes=[t_nea])
    P.ts("dve", nea[:], nea[:], -1.0, None, ALU.mult, reads=[t_nea], writes=[t_nea])
    P.act(scT[:, :, 0], smT[:, R_CCTX:R_CCTX + 16], AF.Silu, reads=[t_smT], writes=[t_scT])
    P.act(scT[:, :, 1], smT[:, R_C:R_C + 16], AF.Silu, reads=[t_smT, t_scT], writes=[t_scT])

    try:
        chk(1)
    except _Stop:
        P.build(); return nc

    def load_w(src2d, ncols=128):
        i = wrot.next()
        P.dma("pool", wring[i][:, :, 0:ncols], src2d.rearrange("(kc p) n -> p kc n", p=128),
              writes=[t_wring[i]])
        return i

    for l in range(depth):
        b = sml.next()
        for blk in range(48):
            wi = load_w(w_mod[l, :, blk * 128:(blk + 1) * 128])
            for kc in range(KC):
                P.mm(banks[b][:, blk * 2:blk * 2 + 2], wring[wi][:, kc, :], scT[:, kc, :],
                     start=(kc == 0), stop=(kc == KC - 1),
                     reads=[t_wring[wi], t_scT], writes=[t_bank[b]])
        P.tt("dve", modT[:, l, :, :], banks[b][:, 0:96].rearrange("p (a c) -> p a c", c=2),
             smT[:, R_BMOD + l * 48:R_BMOD + (l + 1) * 48].unsqueeze(2).to_broadcast([128, 48, 2]),
             ALU.add, reads=[t_bank[b], t_smT], writes=[t_modT])
        for gi in range(2):
            P.ts("dve", Amod[:, l, gi, :], modT[:, l, 16:32, gi], 1.0, None, ALU.add,
                 reads=[t_modT], writes=[t_Amod])
            P.tt("dve", Amod[:, l, gi, :], Amod[:, l, gi, :], smT[:, R_NORM + l * 16:R_NORM + (l + 1) * 16],
                 ALU.mult, reads=[t_Amod, t_smT], writes=[t_Amod])

    try:
        chk(2)
    except _Stop:
        P.build(); return nc

    def rsqrt_from_bank(out_ap, t_out, b, ncols, scale, reads_extra=()):
        P.act(out_ap, banks[b][:, 0:ncols], AF.Sqrt, bias=EPS, scale=scale,
              reads=[t_bank[b]], writes=[t_out])
        P.recip(out_ap, out_ap, reads=[t_out], writes=[t_out])

    def norm_pass(g, src, t_src, l, final):
        G = GROUPS[g]; T = G["T"]; NT = T // 512
        gi = 0 if g == "p" else 1
        for nt in range(NT):
            tsl = slice(nt * 512, (nt + 1) * 512)
            b = big.next()
            for kc in range(KC):
                st, t_st = r_st.next()
                P.dma("sp", st[:], src[kc, :, tsl], reads=[t_src[kc][nt]], writes=[t_st])
                si = sqrot.next()
                P.act(sqs[si][:], st[:], AF.Square, reads=[t_st], writes=[t_sqs[si]])
                P.mm(banks[b][:], ones_b, sqs[si][:], start=(kc == 0), stop=(kc == KC - 1),
                     reads=[t_sqs[si], t_cb], writes=[t_bank[b]], inc=True)
            rsqrt_from_bank(rstd, t_rstd, b, 512, 1.0 / D)
            for kc in range(KC):
                st, t_st = r_st.next()
                P.dma("sp", st[:], src[kc, :, tsl], reads=[t_src[kc][nt]], writes=[t_st])
                P.tt("dve", st[:], st[:], rstd, ALU.mult, reads=[t_st, t_rstd], writes=[t_st])
                if not final:
                    P.act(hT[:, kc, tsl], st[:], AF.Identity,
                          bias=modT[:, l, kc, gi:gi + 1], scale=Amod[:, l, gi, kc:kc + 1],
                          reads=[t_st, t_modT, t_Amod], writes=[t_hT[nt]])
                else:
                    P.act(st[:], st[:], AF.Identity, scale=smT[:, R_FN + kc:R_FN + kc + 1],
                          reads=[t_st, t_smT], writes=[t_st])
                    P.dma("sp", yout[g][kc, :, tsl], st[:], reads=[t_st])

    def proj(T, wi, evac):
        NT = T // 512
        for nt in range(NT):
            b = big.next()
            for kc in range(KC):
                P.mm(banks[b][:], wring[wi][:, kc, :], hT[:, kc, nt * 512:(nt + 1) * 512],
                     start=(kc == 0), stop=(kc == KC - 1),
                     reads=[t_wring[wi], t_hT[nt]], writes=[t_bank[b]])
            evac(nt, b)

    def conv_silu(G, l, blk, out_ap, t_out, out_is_cv):
        T, L, nseq = G["T"], G["L"], G["nseq"]
        base = R_CONV + l * 108
        w0 = smT[:, base + blk:base + blk + 1]
        w1 = smT[:, base + 36 + blk:base + 36 + blk + 1]
        w2 = smT[:, base + 72 + blk:base + 72 + blk + 1]
        P.ts("dve", cv[:, 0:T], raw[:, 0:T], w1, None, ALU.mult, reads=[t_raw, t_smT], writes=[t_cv])
        for s in range(nseq):
            a, e = s * L, (s + 1) * L
            P.stt(cv[:, a + 1:e], raw[:, a:e - 1], w0, cv[:, a + 1:e], ALU.mult, ALU.add,
                  reads=[t_raw, t_smT, t_cv], writes=[t_cv])
            P.stt(cv[:, a:e - 1], raw[:, a + 1:e], w2, cv[:, a:e - 1], ALU.mult, ALU.add,
                  reads=[t_raw, t_smT, t_cv], writes=[t_cv])
        if out_is_cv:
            P.act(cv[:, 0:T], cv[:, 0:T], AF.Silu, reads=[t_cv], writes=[t_cv])
        else:
            P.act(out_ap, cv[:, 0:T], AF.Silu, reads=[t_cv], writes=[t_out])

    def l2norm(T, out_ap, t_out, scale):
        NT = T // 512
        P.act(scrB[:, 0:T], cv[:, 0:T], AF.Square, reads=[t_cv], writes=[t_scrB])
        for nt in range(NT):
            tsl = slice(nt * 512, (nt + 1) * 512)
            b = big.next()
            P.mm(banks[b][:], ones_b, scrB[:, tsl], reads=[t_scrB, t_cb], writes=[t_bank[b]])
            st, t_st = r_st.next()
            P.act(st[:], banks[b][:], AF.Sqrt, bias=EPS, scale=1.0, reads=[t_bank[b]], writes=[t_st])
            P.recip(st[:], st[:], reads=[t_st], writes=[t_st])
            o3 = kqf[:, nt * 1024:(nt + 1) * 1024].rearrange("p (a c) -> p a c", c=256)
            o3 = o3[:, :, 128:256] if out_ap == "q" else o3[:, :, 0:128]
            P.stt(o3, cv[:, tsl].rearrange("p (a c) -> p a c", c=128), scale,
                  st[:].rearrange("p (a c) -> p a c", c=128), ALU.mult, ALU.mult,
                  reads=[t_cv, t_st], writes=[t_out])

    def layer(g, l, src, t_src, dst, t_dst):
        G = GROUPS[g]; T, L, nseq, grid = G["T"], G["L"], G["nseq"], G["grid"]
        NT = T // 512; NC = T // 128; NCs = L // 128
        gi = 0 if g == "p" else 1
        W = w_in[l]
        norm_pass(g, src, t_src, l, False)
        chk(3)
        wi = load_w(W[:, 7168:7216], 48)
        for c0 in range(0, NC, 8):
            b = sml.next()
            for c in range(c0, min(NC, c0 + 8)):
                for kc in range(KC):
                    P.mm(banks[b][:, (c - c0) * 48:(c - c0 + 1) * 48], hT[:, kc, c * 128:(c + 1) * 128],
                         wring[wi][:, kc, 0:48], start=(kc == 0), stop=(kc == KC - 1),
                         reads=[t_wring[wi], t_hT[c // 4]], writes=[t_bank[b]])
            n = min(NC, c0 + 8) - c0
            P.copy("dve", abt[:, c0:c0 + n, :], banks[b][:, 0:n * 48].rearrange("p (a c) -> p a c", c=48),
                   reads=[t_bank[b]], writes=[t_abt])
        dtb = albc[:, 96 + l * 24:96 + (l + 1) * 24].unsqueeze(1).to_broadcast([128, NC, 24])
        neab = nea[:, l * 24:(l + 1) * 24].unsqueeze(1).to_broadcast([128, NC, 24])
        P.tt("dve", lg[:, 0:NC, :], abt[:, 0:NC, 0:24], dtb, ALU.add, reads=[t_abt, t_albc], writes=[t_lg])
        P.act(lg[:, 0:NC, :], lg[:, 0:NC, :], AF.Exp, reads=[t_lg], writes=[t_lg])
        P.act(lg[:, 0:NC, :], lg[:, 0:NC, :], AF.Ln, bias=1.0, reads=[t_lg], writes=[t_lg])
        P.tt("dve", lg[:, 0:NC, :], lg[:, 0:NC, :], neab, ALU.mult, reads=[t_lg, t_nea], writes=[t_lg])
        P.act(nbeta[:, 0:NC, :], abt[:, 0:NC, 24:48], AF.Exp, scale=-1.0, reads=[t_abt], writes=[t_nbeta])
        P.ts("dve", nbeta[:, 0:NC, :], nbeta[:, 0:NC, :], 1.0, None, ALU.add, reads=[t_nbeta], writes=[t_nbeta])
        P.recip(nbeta[:, 0:NC, :], nbeta[:, 0:NC, :], reads=[t_nbeta], writes=[t_nbeta])
        P.ts("dve", nbeta[:, 0:NC, :], nbeta[:, 0:NC, :], -1.0, None, ALU.mult, reads=[t_nbeta], writes=[t_nbeta])
        for c0 in range(0, NC, 16):
            b = sml.next(); b2 = sml.next()
            for c in range(c0, min(NC, c0 + 16)):
                o = (c - c0) * 24
                P.mm(banks[b][:, o:o + 12], U_f, lg[:, c, 0:12], reads=[t_c32, t_lg], writes=[t_bank[b]])
                P.mm(banks[b][:, o + 12:o + 24], L_f, lg[:, c, 12:24], reads=[t_c32, t_lg], writes=[t_bank[b]])
                P.mm(banks[b2][:, o:o + 24], ones_f, lg[:, c, :], reads=[t_c32, t_lg], writes=[t_bank[b2]])
            n = min(NC, c0 + 16) - c0
            P.copy("dve", gcs[:, c0:c0 + n, :], banks[b][:, 0:n * 24].rearrange("p (a c) -> p a c", c=24),
                   reads=[t_bank[b]], writes=[t_gcs])
            P.tt("dve", edl[:, c0:c0 + n, :], banks[b2][:, 0:n * 24].rearrange("p (a c) -> p a c", c=24),
                 gcs[:, c0:c0 + n, :], ALU.subtract, reads=[t_bank[b2], t_gcs], writes=[t_edl])
            P.act(egl[:, c0:c0 + n, :], banks[b2][:, 0:n * 24].rearrange("p (a c) -> p a c", c=24), AF.Exp,
                  reads=[t_bank[b2]], writes=[t_egl])
        P.act(edl[:, 0:NC, :], edl[:, 0:NC, :], AF.Exp, reads=[t_edl], writes=[t_edl])
        P.act(eg[:, 0:NC, :], gcs[:, 0:NC, :], AF.Exp, reads=[t_gcs], writes=[t_eg])

        chk(4)
        for h in range(NH):
            wq = load_w(W[:, 1024 + h * 128:1024 + (h + 1) * 128])
            wk = load_w(W[:, 2560 + h * 128:2560 + (h + 1) * 128])
            wv = load_w(W[:, 4096 + h * 128:4096 + (h + 1) * 128])
            wz = load_w(W[:, 5632 + h * 128:5632 + (h + 1) * 128])

            def ev_raw(nt, b):
                P.copy("act", raw[:, nt * 512:(nt + 1) * 512], banks[b][:], reads=[t_bank[b]], writes=[t_raw])

            def ev_z(nt, b):
                P.act(zs[:, nt * 512:(nt + 1) * 512], banks[b][:], AF.Silu, reads=[t_bank[b]], writes=[t_zs])

            proj(T, wq, ev_raw)
            conv_silu(G, l, h, None, None, True)
            l2norm(T, "q", t_kq, 128.0 ** -0.5)
            proj(T, wk, ev_raw)
            conv_silu(G, l, 12 + h, None, None, True)
            l2norm(T, "k", t_kq, 1.0)
            proj(T, wv, ev_raw)
            conv_silu(G, l, 24 + h, vT[:, 0:T], t_vT, False)
            proj(T, wz, ev_z)
            chk(5)
            for c0 in range(0, NC, 8):
                n = min(NC, c0 + 8) - c0
                b = sml.next(); bb = banks[b][:].bitcast(BF16)
                for c in range(c0, c0 + n):
                    P.tr(bb[:, (c - c0) * 128:(c - c0 + 1) * 128], kch(c), ident_b,
                         reads=[t_kT, t_cb], writes=[t_bank[b]])
                src3 = bb[:, 0:n * 128].rearrange("p (a c) -> p a c", c=128)
                for vi, (arr, t_arr, col) in enumerate(((eg, t_eg, h), (eg, t_eg, 12 + h),
                                                       (edl, t_edl, h), (edl, t_edl, 12 + h))):
                    P.tt("dve", kvar[vi][:, c0:c0 + n, :], src3,
                         arr[:, c0:c0 + n, col:col + 1].to_broadcast([128, n, 128]), ALU.mult,
                         reads=[t_bank[b], t_arr], writes=[t_kvar[vi]])
                b = sml.next(); bb = banks[b][:].bitcast(BF16)
                for c in range(c0, c0 + n):
                    P.tr(bb[:, (c - c0) * 128:(c - c0 + 1) * 128], vT[:, c * 128:(c + 1) * 128], ident_b,
                         reads=[t_vT, t_cb], writes=[t_bank[b]])
                P.copy("act", vtok[:, c0:c0 + n, :], bb[:, 0:n * 128].rearrange("p (a c) -> p a c", c=128),
                       reads=[t_bank[b]], writes=[t_vtok])

            chk(6)
            written = set()
            units = []
            for sq_ in range(nseq):
                for i in range(NCs):
                    units.append((sq_ * NCs + i, 0, i == 0, i == NCs - 1, sq_))
                    units.append((sq_ * NCs + NCs - 1 - i, 1, i == 0, i == NCs - 1, sq_))
            batches = [units[k:k + 4] for k in range(0, len(units), 4)]

            def prep_stages(bi):
                batch = batches[bi]
                par = bi % 2
                st_ = {}

                def sA1():
                    for j, (c, d, first, last, seq) in enumerate(batch):
                        hd = d * 12 + h
                        csl = slice(c * 128, (c + 1) * 128)
                        bk = sml.next()
                        st_[j] = bk
                        P.mm(banks[bk][:, 0:256], kch(c), kqf[:, c * 256:(c + 1) * 256], reads=[t_kq], writes=[t_bank[bk]])
                        P.mm(banks[bk][:, 256:384], lg[:, c, hd:hd + 1].to_broadcast([128, 128]),
                             U_f if d == 0 else L_f, reads=[t_lg, t_c32], writes=[t_bank[bk]])
                        Dm, t_Dm = U_Dm[j]
                        P.stt(Dm[:], banks[bk][:, 256:384], gcs[:, c, hd:hd + 1], msk[:, 2 * d, :],
                              ALU.subtract, ALU.add, reads=[t_bank[bk], t_gcs, t_msk], writes=[t_Dm])

                def sA2():
                    for j, (c, d, first, last, seq) in enumerate(batch):
                        Dm, t_Dm = U_Dm[j]; Es, t_Es = U_Es[j]; Ei, t_Ei = U_Ei[j]
                        P.act(Es[:], Dm[:], AF.Exp, reads=[t_Dm], writes=[t_Es])
                        P.tt("pool", Ei[:], Es[:], ident_f, ALU.add, reads=[t_Es, t_c32], writes=[t_Ei])

                def sA3():
                    for j, (c, d, first, last, seq) in enumerate(batch):
                        hd = d * 12 + h
                        bk = st_[j]
                        Es, t_Es = U_Es[j]; Ei, t_Ei = U_Ei[j]
                        Pm, t_Pm = U_P[j]; At, t_At = U_At[par][j]
                        P.stt(Pm, banks[bk][:, 0:128], nbeta[:, c, hd:hd + 1], Es[:], ALU.mult, ALU.mult,
                              reads=[t_bank[bk], t_nbeta, t_Es], writes=[t_Pm])
                        P.tt("dve", At[:], banks[bk][:, 128:256], Ei[:], ALU.mult,
                             reads=[t_bank[bk], t_Ei], writes=[t_At])

                def sB():
                    for j in range(len(batch)):
                        Pm, t_Pm = U_P[j]; PTm, t_PTm = U_PT[j]; R, t_R = U_R[j]
                        b = sml.next(); bb = banks[b][:].bitcast(BF16)
                        P.tr(bb[:, 0:128], Pm, ident_b, reads=[t_Pm, t_cb], writes=[t_bank[b]])
                        P.copy("act", PTm[:], bb[:, 0:128], reads=[t_bank[b]], writes=[t_PTm])
                        P.tt("pool", R, Pm, ident_b, ALU.add, reads=[t_Pm, t_cb], writes=[t_R])

                def mkC(k):
                    def sC():
                        for j in range(len(batch)):
                            Pm, t_Pm = U_P[j]; PTm, t_PTm = U_PT[j]; R, t_R = U_R[j]
                            b = sml.next()
                            st_[("c", j)] = b
                            if k == 0:
                                P.mm(banks[b][:, 0:128], PTm[:], Pm, reads=[t_Pm, t_PTm], writes=[t_bank[b]])
                                P.mm(banks[b][:, 256:384], Pm, PTm[:], reads=[t_Pm, t_PTm], writes=[t_bank[b]])
                            elif k < 6:
                                P.mm(banks[b][:, 0:256], PTm[:], U_PRt[j][:, 0:256], reads=[t_Pm, t_PTm, t_R],
                                     writes=[t_bank[b]])
                                P.mm(banks[b][:, 256:384], Pm, PTm[:], reads=[t_Pm, t_PTm], writes=[t_bank[b]])
                            else:
                                P.mm(banks[b][:, 128:256], PTm[:], R, reads=[t_PTm, t_R], writes=[t_bank[b]])
                        for j in range(len(batch)):
                            Pm, t_Pm = U_P[j]; PTm, t_PTm = U_PT[j]; R, t_R = U_R[j]
                            b = st_[("c", j)]
                            if k >= 1:
                                P.tt("dve", R, banks[b][:, 128:256], R, ALU.add, reads=[t_bank[b], t_R], writes=[t_R])
                            if k < 6:
                                P.copy("act" if (k % 2 == 1) else "dve", Pm, banks[b][:, 0:128],
                                       reads=[t_bank[b]], writes=[t_Pm])
                                P.copy("act", PTm[:], banks[b][:, 256:384], reads=[t_bank[b]], writes=[t_PTm])
                    return [sC]

                def sD():
                    for j, (c, d, first, last, seq) in enumerate(batch):
                        R, t_R = U_R[j]
                        b = sml.next()
                        st_[("d", j)] = b
                        P.mm(banks[b][:, 0:128], R, vtok[:, c, :], reads=[t_R, t_vtok], writes=[t_bank[b]])
                        P.mm(banks[b][:, 128:256], kvar[d][:, c, :], R, reads=[t_R, t_kvar[d]], writes=[t_bank[b]])
                    for j, (c, d, first, last, seq) in enumerate(batch):
                        hd = d * 12 + h
                        b = st_[("d", j)]
                        upb, t_upb = U_upb[par][j]; wpT, t_wpT = U_wpT[par][j]
                        P.act(upb[:], banks[b][:, 0:128], AF.Identity, scale=nbeta[:, c, hd:hd + 1],
                              reads=[t_bank[b], t_nbeta], writes=[t_upb])
                        P.copy("dve", wpT[:], banks[b][:, 128:256], reads=[t_bank[b]], writes=[t_wpT])

                stages = [sA1, sA2, sA3, sB]
                for lev in range(0, 7):
                    stages += mkC(lev)
                stages.append(sD)
                return stages

            def rec_step(bi, j):
                c, d, first, last, seq = batches[bi][j]
                par = bi % 2
                hd = d * 12 + h
                csl = slice(c * 128, (c + 1) * 128)
                At, t_At = U_At[par][j]; upb, t_upb = U_upb[par][j]; wpT, t_wpT = U_wpT[par][j]
                if first:
                    if grid:
                        P.dma("sp", S[:, hd, :], s0_in[l, d, h], writes=[t_S[hd]])
                        P.copy("act", Sb[:, hd, :], S[:, hd, :], reads=[t_S[hd]], writes=[t_Sb[hd]])
                    else:
                        P.memset("pool", S[:, hd, :], 0.0, writes=[t_S[hd]])
                        P.memset("pool", Sb[:, hd, :], 0.0, writes=[t_Sb[hd]])
                b = sml.next()
                P.mm(banks[b][:, 0:128], wpT[:], Sb[:, hd, :], reads=[t_wpT, t_Sb[hd]], writes=[t_bank[b]])
                P.mm(banks[b][:, 128:256], qch(c), Sb[:, hd, :], reads=[t_qT, t_Sb[hd]], writes=[t_bank[b]])
                vn, t_vn = r_vn.next(); o1, t_o1 = r_o1.next()
                P.stt(vn[:], banks[b][:, 0:128], nbeta[:, c, hd:hd + 1], upb[:], ALU.mult, ALU.subtract,
                      reads=[t_bank[b], t_nbeta, t_upb], writes=[t_vn])
                P.act(o1[:], banks[b][:, 128:256], AF.Identity, scale=eg[:, c, hd:hd + 1],
                      reads=[t_bank[b], t_eg], writes=[t_o1])
                b2 = sml.next()
                P.mm(banks[b2][:, 0:128], At[:], vn[:], reads=[t_At, t_vn], writes=[t_bank[b2]])
                P.mm(banks[b2][:, 128:256], kvar[2 + d][:, c, :], vn[:], reads=[t_kvar[2 + d], t_vn],
                     writes=[t_bank[b2]])
                P.stt(S[:, hd, :], S[:, hd, :], egl[:, c, hd:hd + 1], banks[b2][:, 128:256], ALU.mult, ALU.add,
                      reads=[t_S[hd], t_egl, t_bank[b2]], writes=[t_S[hd]])
                if not last:
                    P.copy("act", Sb[:, hd, :], S[:, hd, :], reads=[t_S[hd]], writes=[t_Sb[hd]])
                if c in written:
                    P.tt("dve", o1[:], banks[b2][:, 0:128], o1[:], ALU.add, reads=[t_bank[b2], t_o1], writes=[t_o1])
                    P.tt("pool", obuf[:, c, :], obuf[:, c, :], o1[:], ALU.add, reads=[t_o1, t_obuf[c]],
                         writes=[t_obuf[c]])
                else:
                    P.tt("dve", obuf[:, c, :], banks[b2][:, 0:128], o1[:], ALU.add,
                         reads=[t_bank[b2], t_o1], writes=[t_obuf[c]])
                    written.add(c)
                if last and not grid:
                    P.dma("sp", st_out[seq, l, d, h], S[:, hd, :], reads=[t_S[hd]])

            for stg in prep_stages(0):
                stg()
            for bi in range(len(batches)):
                nxt = prep_stages(bi + 1) if bi + 1 < len(batches) else []
                nrec = len(batches[bi])
                per = (len(nxt) + nrec - 1) // nrec if nxt else 0
                k = 0
                for j in range(nrec):
                    rec_step(bi, j)
                    for _ in range(per):
                        if k < len(nxt):
                            nxt[k](); k += 1
                while k < len(nxt):
                    nxt[k](); k += 1

            allo = [t_obuf[c] for c in range(NC)]
            for c0 in range(0, NC, 4):
                n = min(NC, c0 + 4) - c0
                st, t_st = r_st.next()
                st3 = st[:, 0:n * 128].rearrange("p (a c) -> p a c", c=128)
                P.tt("pool", st3, obuf[:, c0:c0 + n, :], obuf[:, c0:c0 + n, :], ALU.mult,
                     reads=allo[c0:c0 + n], writes=[t_st])
                P.reduce_sum(ssq[:, c0:c0 + n], st3, reads=[t_st], writes=[t_ssq])
            P.act(ssq[:, 0:NC], ssq[:, 0:NC], AF.Sqrt, bias=EPS, scale=1.0 / 128, reads=[t_ssq], writes=[t_ssq])
            P.recip(ssq[:, 0:NC], ssq[:, 0:NC], reads=[t_ssq], writes=[t_ssq])
            on3 = scrB[:, 0:T].rearrange("p (a c) -> p a c", c=128)
            P.tt("dve", on3, obuf[:, 0:NC, :], ssq[:, 0:NC].unsqueeze(2).to_broadcast([128, NC, 128]), ALU.mult,
                 reads=allo + [t_ssq], writes=[t_scrB])
            for c0 in range(0, NC, 8):
                n = min(NC, c0 + 8) - c0
                b = sml.next(); bb = banks[b][:].bitcast(BF16)
                for c in range(c0, c0 + n):
                    P.tr(bb[:, (c - c0) * 128:(c - c0 + 1) * 128], on3[:, c, :], ident_b,
                         reads=[t_scrB, t_cb], writes=[t_bank[b]])
                P.stt(zs[:, c0 * 128:(c0 + n) * 128], bb[:, 0:n * 128], smT[:, R_GN + l:R_GN + l + 1],
                      zs[:, c0 * 128:(c0 + n) * 128], ALU.mult, ALU.mult,
                      reads=[t_bank[b], t_smT, t_zs], writes=[t_zs])
            P.dma("sp", yscr[4 + h, :, 0:T], zs[:, 0:T], reads=[t_zs], writes=[t_yscr[4 + h]])

        chk(8)
        for fg in range(4):
            wu = load_w(W[:, fg * 128:(fg + 1) * 128])
            wzf = load_w(W[:, 512 + fg * 128:512 + (fg + 1) * 128])

            def ev_u(nt, b):
                P.copy("act", qT[:, nt * 512:(nt + 1) * 512], banks[b][:], reads=[t_bank[b]], writes=[t_qT])

            def ev_zf(nt, b):
                P.act(zs[:, nt * 512:(nt + 1) * 512], banks[b][:], AF.Silu, reads=[t_bank[b]], writes=[t_zs])

            proj(T, wu, ev_u)
            proj(T, wzf, ev_zf)
            if not grid:
                fscale = float((L * 128) ** -0.5)
                for s in range(nseq):
                    b = sml.next(); bb = banks[b][:].bitcast(BF16)
                    for j in range(2):
                        c = s * 2 + j
                        P.tr(bb[:, j * 128:(j + 1) * 128], qT[:, c * 128:(c + 1) * 128], ident_b,
                             reads=[t_qT, t_cb], writes=[t_bank[b]])
                    P.copy("act", kT[:, 0:256], bb[:, 0:256], reads=[t_bank[b]], writes=[t_kT])
                    b = sml.next()
                    for j in range(2):
                        P.mm(banks[b][:], kT[:, j * 128:(j + 1) * 128], dp[:, j, :], start=(j == 0), stop=(j == 1),
                             reads=[t_kT, t_dp], writes=[t_bank[b]])
                    P.copy("dve", vT[:, 0:512], banks[b][:], reads=[t_bank[b]], writes=[t_vT])
                    b = sml.next()
                    P.mm(banks[b][:, 0:256], dc[:, 0, :], vT[:, 0:256], start=True, stop=False,
                         reads=[t_dc, t_vT], writes=[t_bank[b]])
                    P.mm(banks[b][:, 0:256], dc[:, 1, :], vT[:, 256:512], start=False, stop=True,
                         reads=[t_dc, t_vT], writes=[t_bank[b]])
                    P.stt(zs[:, s * 256:(s + 1) * 256], banks[b][:, 0:256], fscale, zs[:, s * 256:(s + 1) * 256],
                          ALU.mult, ALU.mult, reads=[t_bank[b], t_zs], writes=[t_zs])
            else:
                fscale = float((2048 * 128) ** -0.5)
                u4 = qT[:, 0:2048].rearrange("p (r w) -> p w r", w=64)
                zre = kvar[0][:].rearrange("p a c -> p (a c)"); zim = kvar[1][:].rearrange("p a c -> p (a c)")
                z2re = kvar[2][:].rearrange("p a c -> p (a c)"); z2im = kvar[3][:].rearrange("p a c -> p (a c)")
                P.copy("dve", vT[:, 0:2048].rearrange("p (w r) -> p w r", w=64), u4, reads=[t_qT], writes=[t_vT])
                for j in range(16):
                    b = sml.next(); bb = banks[b][:].bitcast(BF16)
                    P.tr(bb[:, 0:128], vT[:, j * 128:(j + 1) * 128], ident_b,
                         reads=[t_vT, t_cb], writes=[t_bank[b]])
                    tk, t_tk = r_P.next()
                    P.copy("act", tk[:, 0:128], bb[:, 0:128], reads=[t_bank[b]], writes=[t_tk])
                    b = sml.next()
                    P.mm(banks[b][:, 0:256], tk[:, 0:128], dsm[:, 0, :], reads=[t_tk, t_dsm], writes=[t_bank[b]])
                    P.copy("dve", zre[:, j * 128:(j + 1) * 128], banks[b][:, 0:128], reads=[t_bank[b]], writes=[t_kvar[0]])
                    P.copy("act", zim[:, j * 128:(j + 1) * 128], banks[b][:, 128:256], reads=[t_bank[b]], writes=[t_kvar[1]])
                zre3 = zre.rearrange("p (w k) -> p k w", k=32); zim3 = zim.rearrange("p (w k) -> p k w", k=32)
                P.copy("dve", kT[:, 0:2048].rearrange("p (k w) -> p k w", k=32), zre3, reads=[t_kvar[0]], writes=[t_kT])
                P.copy("pool", scrB[:, 0:2048].rearrange("p (k w) -> p k w", k=32), zim3, reads=[t_kvar[1]], writes=[t_scrB])
                for j in range(16):
                    b = sml.next(); bb = banks[b][:].bitcast(BF16)
                    P.tr(bb[:, 0:128], kT[:, j * 128:(j + 1) * 128], ident_b,
                         reads=[t_kT, t_cb], writes=[t_bank[b]])
                    P.tr(bb[:, 128:256], scrB[:, j * 128:(j + 1) * 128], ident_b,
                         reads=[t_scrB, t_cb], writes=[t_bank[b]])
                    tk, t_tk = r_P.next(); tk2, t_tk2 = r_PT.next()
                    P.copy("act", tk[:, 0:128], bb[:, 0:128], reads=[t_bank[b]], writes=[t_tk])
                    P.copy("dve", tk2[:], bb[:, 128:256], reads=[t_bank[b]], writes=[t_tk2])
                    b = sml.next()
                    P.mm(banks[b][:, 0:256], tk[:, 0:128], dsm[:, 1, :], start=True, stop=False,
                         reads=[t_tk, t_dsm], writes=[t_bank[b]])
                    P.mm(banks[b][:, 0:256], tk2[:], dsm[:, 2, :], start=False, stop=True,
                         reads=[t_tk2, t_dsm], writes=[t_bank[b]])
                    P.copy("dve", z2re[:, j * 128:(j + 1) * 128], banks[b][:, 0:128], reads=[t_bank[b]], writes=[t_kvar[2]])
                    P.copy("act", z2im[:, j * 128:(j + 1) * 128], banks[b][:, 128:256], reads=[t_bank[b]], writes=[t_kvar[3]])
                for nt in range(4):
                    tsl = slice(nt * 512, (nt + 1) * 512)
                    b = big.next()
                    P.mm(banks[b][:], dc[:, 0, :], z2re[:, tsl], start=True, stop=False,
                         reads=[t_dc, t_kvar[2]], writes=[t_bank[b]])
                    P.mm(banks[b][:], dc[:, 1, :], z2im[:, tsl], start=False, stop=True,
                         reads=[t_dc, t_kvar[3]], writes=[t_bank[b]])
                    P.stt(zs[:, tsl], banks[b][:], fscale, zs[:, tsl], ALU.mult, ALU.mult,
                          reads=[t_bank[b], t_zs], writes=[t_zs])
            P.dma("sp", yscr[fg, :, 0:T], zs[:, 0:T], reads=[t_zs], writes=[t_yscr[fg]])

        chk(9)
        for kc in range(KC):
            for nt in range(NT):
                P.dma("sp", hT[:, kc, nt * 512:(nt + 1) * 512], yscr[kc, :, nt * 512:(nt + 1) * 512],
                      reads=[t_yscr[kc]], writes=[], joins=[t_hT[nt]])
        for j in range(KC):
            wi = load_w(w_out[l][:, j * 128:(j + 1) * 128])

            def ev_o(nt, b, j=j):
                tsl = slice(nt * 512, (nt + 1) * 512)
                st, t_st = r_st.next()
                P.dma("sp", st[:], src[j, :, tsl], reads=[t_src[j][nt]], writes=[t_st])
                P.stt(st[:], banks[b][:], modT[:, l, 32 + j, gi:gi + 1], st[:], ALU.mult, ALU.add,
                      reads=[t_bank[b], t_modT, t_st], writes=[t_st])
                P.dma("sp", dst[j, :, tsl], st[:], reads=[t_st], writes=[t_dst[j][nt]])

            proj(T, wi, ev_o)

    t_yscr = [Trk() for _ in range(KC)]
    try:
        for g in groups:
            NT = GROUPS[g]["T"] // 512
            if g == "p":
                P.dma("sp", dp, dftp, writes=[t_dp])
            else:
                P.dma("sp", dsm, dfts, writes=[t_dsm])
            t_in = [[Trk() for _ in range(NT)] for _ in range(KC)]
            t_sc = [[Trk() for _ in range(NT)] for _ in range(KC)]
            for l in range(depth):
                src, t_src = (xin[g], t_in) if l == 0 else (xscr[g], t_sc)
                layer(g, l, src, t_src, xscr[g], t_sc)
                chk(10)
            norm_pass(g, xscr[g], t_sc, 0, True)
    except _Stop:
        pass
    P.build()
    return nc


def _consts():
    bf = ml_dtypes.bfloat16
    i = np.arange(128)
    ident = np.eye(128, dtype=np.float32)
    U = (i[:, None] <= i[None, :]).astype(np.float32)
    Lw = (i[:, None] >= i[None, :]).astype(np.float32)
    ones = np.ones((128, 128), np.float32)
    cf32 = np.stack([ident, U, Lw, ones], axis=1)
    NEG = -30000.0
    negf = np.where(i[:, None] < i[None, :], 0.0, NEG)
    negb = np.where(i[:, None] > i[None, :], 0.0, NEG)
    masks = np.stack([negf, negf, negb, negb], axis=1).astype(np.float32)
    cbf = np.stack([ident, ones], axis=1).astype(bf)
    ang = 2 * np.pi * np.outer(i, i) / 128.0
    dftc = np.stack([np.cos(ang), np.sin(ang)], axis=1).astype(bf)
    k = np.arange(256)
    dftp = np.zeros((128, 2, 512), np.float64)
    for j in range(2):
        a = 2 * np.pi * np.outer(j * 128 + i, k) / 256.0
        dftp[:, j, 0:256] = np.cos(a)
        dftp[:, j, 256:512] = -np.sin(a)
    dftp = dftp.astype(bf)
    r = np.arange(32)
    a32 = 2 * np.pi * np.outer(r, r) / 32.0
    bd32c = np.kron(np.eye(4), np.cos(a32)); bd32s = np.kron(np.eye(4), np.sin(a32))
    w = np.arange(64)
    a64 = 2 * np.pi * np.outer(w, w) / 64.0
    bd64c = np.kron(np.eye(2), np.cos(a64)); bd64s = np.kron(np.eye(2), np.sin(a64))
    dfts = np.stack([np.concatenate([bd32c, -bd32s], 1), np.concatenate([bd64c, -bd64s], 1),
                     np.concatenate([bd64s, bd64c], 1)], axis=1).astype(bf)
    return dict(cf32=cf32, masks=masks, cbf=cbf, dftc=dftc, dftp=dftp, dfts=dfts)


def _core_inputs(i, depth, groups, x_prompt, x_sample, state_ctx, c, c_ctx, norm_w, w_mod, b_mod, w_in,
                 conv_w, a_log, dt_bias, gnorm_w, w_out, final_norm_w, consts):
    f = np.float32
    m = dict(consts)
    b = i % x_sample.shape[0]
    if "p" in groups:
        xp = np.asarray(x_prompt[2 * i:2 * i + 2], f).reshape(512, D)
        m["xT_p"] = np.ascontiguousarray(xp.T).reshape(KC, 128, 512)
    if "s" in groups:
        m["xT_s"] = np.ascontiguousarray(np.asarray(x_sample[b], f).T).reshape(KC, 128, 2048)
        m["s0"] = np.ascontiguousarray(state_ctx[b][:depth], dtype=f)
    m["w_in"] = w_in; m["w_out"] = w_out; m["w_mod"] = w_mod
    small = np.zeros((768, 128), f)
    for l in range(depth):
        small[R_NORM + l * 16:R_NORM + (l + 1) * 16] = norm_w[l].reshape(16, 128)
        small[R_BMOD + l * 48:R_BMOD + (l + 1) * 48] = b_mod[l].reshape(48, 128)
        small[R_CONV + l * 108:R_CONV + (l + 1) * 108] = conv_w[l].reshape(108, 128)
        small[R_GN + l] = gnorm_w[l]
    small[R_FN:R_FN + 16] = final_norm_w.reshape(16, 128)
    small[R_CCTX:R_CCTX + 16] = c_ctx.reshape(16, 128)
    small[R_C:R_C + 16] = c[b].reshape(16, 128)
    m["small"] = small
    al = np.zeros(192, f)
    al[0:depth * 24] = a_log[:depth].reshape(-1)
    al[96:96 + depth * 24] = dt_bias[:depth].reshape(-1)
    m["albt"] = np.ascontiguousarray(np.broadcast_to(al[None, :], (128, 192)))
    return m


_NC_CACHE = {}


def run_cores(inputs, depth=4, groups=("p", "s"), n_cores=8, runner=None):
    key = (depth, tuple(groups))
    if key not in _NC_CACHE:
        _NC_CACHE[key] = build_program(depth, groups)
    nc = _NC_CACHE[key]
    consts = _consts()
    a = {k: np.asarray(v) for k, v in inputs.items()}
    w_in = np.ascontiguousarray(a["w_in"][:depth], dtype=np.float32)
    w_out = np.ascontiguousarray(a["w_out"][:depth], dtype=np.float32)
    w_mod = np.ascontiguousarray(a["w_mod"][:depth], dtype=np.float32)
    in_maps = [_core_inputs(i, depth, groups, a["x_prompt"], a["x_sample"], a["state_ctx"], a["c"], a["c_ctx"],
                            a["norm_w"], w_mod, a["b_mod"], w_in, a["conv_w"], a["a_log"], a["dt_bias"],
                            a["gnorm_w"], w_out, a["final_norm_w"], consts) for i in range(n_cores)]
    if runner is None:
        res = run_bass_kernel_spmd(nc, in_maps, core_ids=list(range(n_cores))).results
    else:
        res = runner(nc, in_maps)
    return res


def kernel(x_prompt, x_sample, state_ctx, c, c_ctx, norm_w, w_mod, b_mod, w_in,
           conv_w, a_log, dt_bias, gnorm_w, w_out, final_norm_w):
    inputs = dict(x_prompt=x_prompt, x_sample=x_sample, state_ctx=state_ctx, c=c, c_ctx=c_ctx, norm_w=norm_w,
                  w_mod=w_mod, b_mod=b_mod, w_in=w_in, conv_w=conv_w, a_log=a_log, dt_bias=dt_bias,
                  gnorm_w=gnorm_w, w_out=w_out, final_norm_w=final_norm_w)
    res = run_cores(inputs)
    B = np.asarray(x_prompt).shape[0]
    BS = np.asarray(x_sample).shape[0]
    y_prompt = np.zeros((B, 256, D), np.float32)
    y_sample = np.zeros((BS, 2048, D), np.float32)
    state_new = np.zeros((B, 4, 2, NH, 128, 128), np.float32)
    for i in range(8):
        r = res[i]
        y_prompt[2 * i:2 * i + 2] = np.asarray(r["yT_p"]).reshape(D, 512).T.reshape(2, 256, D)
        state_new[2 * i:2 * i + 2] = np.asarray(r["st"])
        if i < BS:
            y_sample[i] = np.asarray(r["yT_s"]).reshape(D, 2048).T
    return (y_prompt, y_sample, state_new)
```
